# Optimizing a Trainium2 kernel written in Bass

```python
import jax, jax.numpy as jnp
from jax import lax
import numpy as np

D_MODEL = 1024
BATCH = 16
SEQ = 4096
DEPTH = 4

GRID_W = 64
CTX_LEN = 256
HEAD_DIM = 64
W_SC = D_MODEL // 4
W_NA = D_MODEL // 2
NA_HEADS = W_NA // HEAD_DIM
W_CF = D_MODEL // 4
D_MIX = W_SC + W_NA + W_CF
SC_K = 3
CF_K = 31
NA_ROWS = 8
NA_COLS = 16
NORM_EPS = 1e-6
LN_EPS = 1e-5
PROJ_SIZES = [W_SC] * 4 + [W_NA] * 4 + [W_CF] * 3
PROJ_SPLITS = [int(s) for s in np.cumsum(PROJ_SIZES)[:-1]]
D_PROJ = int(sum(PROJ_SIZES))

kernel_name = "hybrid_conv_natten_conformer_dit"


def rmsnorm(x, g):
    xf = x.astype(jnp.float32)
    y = xf * lax.rsqrt(jnp.mean(xf * xf, axis=-1, keepdims=True) + NORM_EPS)
    return (y * g.astype(jnp.float32)).astype(x.dtype)


def layernorm(x, g, b):
    xf = x.astype(jnp.float32)
    mu = jnp.mean(xf, axis=-1, keepdims=True)
    var = jnp.mean(jnp.square(xf - mu), axis=-1, keepdims=True)
    y = (xf - mu) * lax.rsqrt(var + LN_EPS)
    return (y * g.astype(jnp.float32) + b.astype(jnp.float32)).astype(x.dtype)


def dwconv(x, w):
    k = w.shape[0]
    return lax.conv_general_dilated(
        x, w[:, None, :].astype(x.dtype), window_strides=(1,),
        padding=[(k // 2, k // 2)], dimension_numbers=('NWC', 'WIO', 'NWC'),
        feature_group_count=x.shape[-1])


def short_conv_branch(z_h, z_b, z_c, z_g, w_conv):
    return z_b * dwconv(z_c * z_h, w_conv) * jax.nn.silu(z_g)


def conformer_branch(z_p, z_q, z_g, w_conv, b_conv, ln_g, ln_b):
    u = z_p * jax.nn.sigmoid(z_q)
    u = dwconv(u, w_conv) + b_conv.astype(u.dtype)
    u = layernorm(u, ln_g, ln_b)
    return jax.nn.silu(u) * jax.nn.silu(z_g)


def heads(t):
    b, l, _ = t.shape
    return t.reshape(b, l, NA_HEADS, HEAD_DIM)


def neighbourhood_attention(q, k, v, kc, vc, rpb):
    b, l, h, dh = q.shape
    rows = l // GRID_W
    kr = min(NA_ROWS, rows)
    scale = HEAD_DIM ** -0.5
    qg = q.reshape(b, rows, GRID_W, h, dh)
    kg = k.reshape(b, rows, GRID_W, h, dh)
    vg = v.reshape(b, rows, GRID_W, h, dh)
    cols = jnp.arange(GRID_W)
    col_start = jnp.clip(cols - NA_COLS // 2, 0, GRID_W - NA_COLS)
    col_idx = col_start[:, None] + jnp.arange(NA_COLS)[None, :]
    col_bias_idx = col_idx - cols[:, None] + (NA_COLS - 1)
    rpb_cols = rpb[:, :, col_bias_idx]

    def row_block(r):
        rs = jnp.clip(r - kr // 2, 0, rows - kr)
        kb = lax.dynamic_slice_in_dim(kg, rs, kr, axis=1)
        vb = lax.dynamic_slice_in_dim(vg, rs, kr, axis=1)
        qr = lax.dynamic_index_in_dim(qg, r, axis=1, keepdims=False)
        kw = kb[:, :, col_idx]
        vw = vb[:, :, col_idx]
        row_bias_idx = rs + jnp.arange(kr) - r + (NA_ROWS - 1)
        bias = jnp.take(rpb_cols, row_bias_idx, axis=1)
        bias = bias.transpose(0, 2, 1, 3).astype(jnp.float32)
        s_win = jnp.einsum('bqhd,brqchd->bhqrc', qr, kw).astype(jnp.float32) * scale + bias[None]
        s_ctx = jnp.einsum('bqhd,bkhd->bhqk', qr, kc).astype(jnp.float32) * scale
        s = jnp.concatenate([s_win.reshape(b, h, GRID_W, kr * NA_COLS), s_ctx], axis=-1)
        p = jax.nn.softmax(s, axis=-1).astype(v.dtype)
        pw = p[..., :kr * NA_COLS].reshape(b, h, GRID_W, kr, NA_COLS)
        pc = p[..., kr * NA_COLS:]
        return (jnp.einsum('bhqrc,brqchd->bqhd', pw, vw)
                + jnp.einsum('bhqk,bkhd->bqhd', pc, vc))

    out = lax.map(row_block, jnp.arange(rows))
    return out.transpose(1, 0, 2, 3, 4).reshape(b, l, h * dh)


def context_attention(q, k, v):
    b, n, h, dh = q.shape
    s = jnp.einsum('bqhd,bkhd->bhqk', q, k).astype(jnp.float32) * (HEAD_DIM ** -0.5)
    p = jax.nn.softmax(s, axis=-1).astype(v.dtype)
    return jnp.einsum('bhqk,bkhd->bqhd', p, v).reshape(b, n, h * dh)


def setup_inputs(seed: int = 0) -> dict:
    key = jax.random.key(seed)
    ks = jax.random.split(key, 16)
    f32 = jnp.float32
    nrm = lambda k, shape, s: (jax.random.normal(k, shape, f32) * s).astype(f32)
    return {
        "x": nrm(ks[0], (BATCH, SEQ, D_MODEL), 1.0),
        "c": nrm(ks[1], (BATCH, D_MODEL), 1.0),
        "ctx": nrm(ks[2], (BATCH, CTX_LEN, D_MODEL), 1.0),
        "c_ctx": nrm(ks[3], (D_MODEL,), 1.0),
        "norm_g": 1.0 + nrm(ks[4], (DEPTH, D_MODEL), 0.02),
        "w_ada": nrm(ks[5], (DEPTH, D_MODEL, 3 * D_MODEL), 0.5 * D_MODEL ** -0.5),
        "b_ada": nrm(ks[6], (DEPTH, 3 * D_MODEL), 0.02),
        "w_in": nrm(ks[7], (DEPTH, D_MODEL, D_PROJ), D_MODEL ** -0.5),
        "conv_sc": nrm(ks[8], (DEPTH, SC_K, W_SC), SC_K ** -0.5),
        "rpb": nrm(ks[9], (DEPTH, NA_HEADS, 2 * NA_ROWS - 1, 2 * NA_COLS - 1), 0.1),
        "conv_cf": nrm(ks[10], (DEPTH, CF_K, W_CF), CF_K ** -0.5),
        "conv_cf_b": nrm(ks[11], (DEPTH, W_CF), 0.02),
        "ln_cf_g": 1.0 + nrm(ks[12], (DEPTH, W_CF), 0.02),
        "ln_cf_b": nrm(ks[13], (DEPTH, W_CF), 0.02),
        "w_out": nrm(ks[14], (DEPTH, D_MIX, D_MODEL), D_MIX ** -0.5),
        "final_g": 1.0 + nrm(ks[15], (D_MODEL,), 0.02),
    }


def reference(x, c, ctx, c_ctx, norm_g, w_ada, b_ada, w_in, conv_sc, rpb,
              conv_cf, conv_cf_b, ln_cf_g, ln_cf_b, w_out, final_g):
    for l in range(DEPTH):
        last = l == DEPTH - 1
        mod = jax.nn.silu(c) @ w_ada[l] + b_ada[l]
        shift, scale, gate = jnp.split(mod, 3, axis=-1)
        mod_c = jax.nn.silu(c_ctx) @ w_ada[l] + b_ada[l]
        shift_c, scale_c, gate_c = jnp.split(mod_c, 3, axis=-1)

        hl = rmsnorm(x, norm_g[l]) * (1.0 + scale[:, None, :]) + shift[:, None, :]
        hc = rmsnorm(ctx, norm_g[l]) * (1.0 + scale_c) + shift_c

        zl = jnp.split(hl @ w_in[l], PROJ_SPLITS, axis=-1)
        zc = jnp.split(hc @ w_in[l], PROJ_SPLITS, axis=-1)
        kc, vc = heads(zc[5]), heads(zc[6])

        ya = short_conv_branch(zl[0], zl[1], zl[2], zl[3], conv_sc[l])
        yb = neighbourhood_attention(heads(zl[4]), heads(zl[5]), heads(zl[6]), kc, vc, rpb[l])
        yb = yb * jax.nn.silu(zl[7])
        yc = conformer_branch(zl[8], zl[9], zl[10], conv_cf[l], conv_cf_b[l], ln_cf_g[l], ln_cf_b[l])
        y = jnp.concatenate([ya, yb, yc], axis=-1) @ w_out[l]

        if not last:
            ca = short_conv_branch(zc[0], zc[1], zc[2], zc[3], conv_sc[l])
            cb = context_attention(heads(zc[4]), kc, vc) * jax.nn.silu(zc[7])
            cc = conformer_branch(zc[8], zc[9], zc[10], conv_cf[l], conv_cf_b[l], ln_cf_g[l], ln_cf_b[l])
            yctx = jnp.concatenate([ca, cb, cc], axis=-1) @ w_out[l]
            ctx = ctx + gate_c * yctx

        x = x + gate[:, None, :] * y
    return rmsnorm(x, final_g)
```

```python
import contextlib
import numpy as np
import concourse.bass as bass
import concourse.mybir as mybir
from concourse.bass_utils import run_bass_kernel_spmd

F32 = mybir.dt.float32
BF16 = mybir.dt.bfloat16
ALU = mybir.AluOpType
AF = mybir.ActivationFunctionType

P = 128
D = 1024
KC = 8
DP = 3840
GW = 64
BT = 256
CTXL = 256
NEG = -30000.0
NORM_EPS = 1e-6
LN_EPS = 1e-5
LLW = 14 * 64


class _Op:
    __slots__ = ("eng", "fn", "deps", "is_dma", "dkey", "ndma", "signal", "token_val")


class Sched:
    ENGS = ("pe", "act", "dve", "pool", "sp")

    def __init__(self, same=("act", "dve", "pool")):
        self.ops = {e: [] for e in self.ENGS}
        self.lw = {}
        self.rd = {}
        self.same = set(same)
        self.dma_count = {}
        self.stores = []

    def add(self, eng, fn, r=(), w=(), dkey=None, ndma=0):
        import os as _os
        self.nadd = getattr(self, "nadd", 0) + 1
        if self.nadd > int(_os.environ.get("KMAXOPS", "100000000")):
            o = _Op()
            o.is_dma = dkey is not None
            o.deps = []
            return o
        o = _Op()
        o.eng = eng
        o.fn = fn
        o.dkey = dkey
        o.ndma = ndma
        o.is_dma = dkey is not None
        o.signal = False
        o.token_val = 0
        deps = {}
        for k in r:
            p = self.lw.get(k)
            if p is not None:
                deps[id(p)] = p
        for k in w:
            p = self.lw.get(k)
            if p is not None:
                deps[id(p)] = p
            rdk = self.rd.get(k)
            if rdk is not None:
                for q in rdk[0].values():
                    deps[id(q)] = q
                for q in rdk[1]:
                    deps[id(q)] = q
        deps.pop(id(o), None)
        o.deps = list(deps.values())
        for k in r:
            rdk = self.rd.get(k)
            if rdk is None:
                rdk = self.rd[k] = ({}, [])
            if o.is_dma:
                rdk[1].append(o)
            else:
                rdk[0][eng] = o
        for k in w:
            self.lw[k] = o
            self.rd[k] = ({}, [])
        if o.is_dma:
            c = self.dma_count.get(dkey, 0) + 16 * ndma
            self.dma_count[dkey] = c
            o.token_val = c
        self.ops[eng].append(o)
        return o

    def finalize(self):
        for e, lst in self.ops.items():
            for o in lst:
                for d in o.deps:
                    if d.is_dma:
                        continue
                    if d.eng != o.eng or (o.eng in self.same) or o.is_dma:
                        d.signal = True
        for e, lst in self.ops.items():
            c = 0
            for o in lst:
                if not o.is_dma and o.signal:
                    c += 1
                    o.token_val = c

    def emit_engine(self, engname, eng, sems, dsems):
        waited = {}
        for o in self.ops[engname]:
            for d in o.deps:
                if d.is_dma:
                    sk, sem, val = ("d", d.dkey), dsems[d.dkey], d.token_val
                else:
                    if d.eng == engname and engname not in self.same and not o.is_dma:
                        continue
                    sk, sem, val = ("e", d.eng), sems[d.eng], d.token_val
                if waited.get(sk, 0) >= val:
                    continue
                eng.wait_ge(sem, val)
                waited[sk] = val
            if o.fn is None:
                continue
            res = o.fn(eng)
            if o.is_dma:
                assert len(res) == o.ndma
                for ins in res:
                    ins.then_inc(dsems[o.dkey], 16)
            elif o.signal:
                res.then_inc(sems[engname], 1)


def small_layout(L):
    off = {}
    c = 0
    for name, n in (("cT", 24), ("normgT", L * 8), ("badaT", L * 24), ("convscT", L * 6),
                    ("convcfT", L * 62), ("cfbT", L * 2), ("lngT", L * 2), ("lnbT", L * 2),
                    ("ident", 128)):
        off[name] = c
        c += n
    return off, c


def build_program(ROWS, L, dbg=None):
    SEQ = ROWS * GW
    NBL = SEQ // BT
    assert NBL >= 3
    nc = bass.Bass("TRN2", target_bir_lowering=False)
    S = Sched()
    soff, NSM = small_layout(L)

    x_d = nc.dram_tensor("x", [2, SEQ, D], F32, kind="ExternalInput")
    ctx_d = nc.dram_tensor("ctx", [2, CTXL, D], F32, kind="ExternalInput")
    small_d = nc.dram_tensor("small", [P, NSM], F32, kind="ExternalInput")
    wada_d = nc.dram_tensor("w_ada", [L, D, 3 * D], F32, kind="ExternalInput")
    win_d = nc.dram_tensor("w_in", [L, D, DP], F32, kind="ExternalInput")
    wout_d = nc.dram_tensor("w_out", [L, D, D], F32, kind="ExternalInput")
    LL_d = nc.dram_tensor("LLb", [L, 8, 2, P, LLW], F32, kind="ExternalInput")
    fg_d = nc.dram_tensor("final_g", [1, D], F32, kind="ExternalInput")
    out_d = nc.dram_tensor("out", [2, SEQ, D], F32, kind="ExternalOutput")
    ctxs_d = nc.dram_tensor("ctxs", [2, CTXL, D], F32)
    dbg_d = {}
    if dbg:
        for name, shape in dbg.items():
            dbg_d[name] = nc.dram_tensor("dbg_" + name, list(shape), F32, kind="ExternalOutput")

    es = contextlib.ExitStack()
    with es:
        def sb(name, shape, dt):
            return es.enter_context(nc.sbuf_tensor(name, shape, dt))

        small = sb("small_sb", [P, NSM], F32)
        ident_bf = sb("ident_bf", [P, P], BF16)
        ones_bf = sb("ones_bf", [P, P], BF16)
        ones256 = sb("ones256", [P, P], F32)
        csT = sb("csT", [P, 24], F32)
        modT = sb("modT", [P, 24, 3], F32)
        Gm = sb("Gm", [P, KC, 3], F32)
        hln = sb("hln", [P, L * 4], F32)
        ssb = sb("ssb", [P, 2, 4], F32)
        rsb = sb("rsb", [P, 2, 4], F32)
        ssf = sb("ssf", [P, 2, 4], F32)
        grep_t = sb("grep_t", [P, 1, P], F32)
        win_sb = sb("win_sb", [P, KC, DP], BF16)
        wout_sb = sb("wout_sb", [P, KC, D], BF16)
        LLs = sb("LLs", [P, 2, LLW], BF16)
        dg_cf = sb("dg_cf", [P, 62, P], BF16)
        dg_sc = sb("dg_sc", [P, 6, P], BF16)
        scrA = sb("scrA", [P, 2, 2048], BF16)
        gate_bc = sb("gate_bc", [P, 3, D], F32)
        finalg_bc = sb("finalg_bc", [P, D], F32)
        xt = sb("xt", [P, 2, D], F32)
        xr = sb("xr", [P, 2, 512], F32)
        xn = sb("xn", [P, 2, D], BF16)
        hT = sb("hT", [P, 2, KC, BT], BF16)
        ub_sc = sb("ub_sc", [P, 3, 2, 288], BF16)
        ub_cf = sb("ub_cf", [P, 3, 2, 288], BF16)
        qT = sb("qTm", [P, 2, 2, 4, BT], BF16)
        kT = sb("kT", [P, 4, 4 * BT], BF16)
        Vr = sb("Vr", [P, 8, 512], BF16)
        kTc = sb("kTc", [P, 4, BT], BF16)
        Vc = sb("Vc", [P, 2, 512], BF16)
        ga = sb("ga", [P, 2, 4, BT], BF16)
        bg = sb("bg", [P, 2, 2, BT], BF16)
        cg = sb("cg", [P, 2, 2, BT], BF16)
        yT = sb("yT", [P, KC, BT], BF16)
        NTMP = 4
        tmp = sb("tmp", [P, NTMP, 512], F32)
        lnt = sb("lnt", [P, 2, 512], F32)
        ps = es.enter_context(nc.psum_tensor("ps", [P, 8, 512], F32))

        ident_f = small[:, soff["ident"]:soff["ident"] + P]

        sems = {e: es.enter_context(nc.semaphore("sem_" + e)) for e in ("pe", "act", "dve", "pool")}
        dsems = {}

        def dsem(key):
            if key not in dsems:
                dsems[key] = es.enter_context(nc.semaphore("ds_" + "_".join(str(k) for k in key)))
            return key

        cnt = {"tmp": 0, "gp": 0, "sps": 0, "pT": 0, "LL": 0, "xt": 0, "xr": 0, "wada": 0, "grep": 0}

        def nxt(name, n):
            i = cnt[name] % n
            cnt[name] += 1
            return i

        def new_tmp():
            i = nxt("tmp", NTMP)
            return tmp[:, i, :], ("tmp", i)

        def new_gp():
            i = nxt("gp", 4)
            return ps[:, i, :], ("gp", i)

        def new_sps():
            i = nxt("sps", 2)
            return ps[:, 4 + 2 * i:6 + 2 * i, :], ("sps", i)

        S.add("sp", lambda e: [e.dma_start(out=small[:, :], in_=small_d.ap()[:, :])],
              w=[("small",)], dkey=dsem(("small",)), ndma=1)
        S.add("sp", lambda e: [e.dma_start(out=finalg_bc[:, :], in_=fg_d.ap()[0:1, :].partition_broadcast(P))],
              w=[("finalg",)], dkey=dsem(("finalg",)), ndma=1)
        S.add("pool", lambda e: e.memset(ones_bf[:, :], 1.0), w=[("ones_bf",)])
        S.add("pool", lambda e: e.memset(ones256[:, :], 1.0 / 256.0), w=[("ones256",)])
        S.add("dve", lambda e: e.tensor_copy(out=ident_bf[:, :], in_=ident_f), r=[("small",)], w=[("ident_bf",)])
        cT_ap = small[:, soff["cT"]:soff["cT"] + 24]
        S.add("act", lambda e: e.activation(out=csT[:, :], in_=cT_ap, func=AF.Tanh, scale=0.5),
              r=[("small",)], w=[("csT",)])
        S.add("dve", lambda e: e.scalar_tensor_tensor(out=csT[:, :], in0=csT[:, :], scalar=1.0, in1=cT_ap,
                                                     op0=ALU.add, op1=ALU.mult), r=[("small",), ("csT",)], w=[("csT",)])
        S.add("dve", lambda e: e.tensor_scalar(out=csT[:, :], in0=csT[:, :], scalar1=0.5, scalar2=None, op0=ALU.mult),
              r=[("csT",)], w=[("csT",)])
        S.add("dve", lambda e: e.tensor_scalar(out=hln[:, 0:L * 2], in0=small[:, soff["lngT"]:soff["lngT"] + L * 2],
                                               scalar1=0.5, scalar2=None, op0=ALU.mult), r=[("small",)], w=[("hln",)])
        S.add("dve", lambda e: e.tensor_scalar(out=hln[:, L * 2:L * 4], in0=small[:, soff["lnbT"]:soff["lnbT"] + L * 2],
                                               scalar1=0.5, scalar2=None, op0=ALU.mult), r=[("small",), ("hln",)], w=[("hln",)])
        for sl_ in range(2):
            for e2_ in range(2):
                S.add("pool", (lambda e, sl_=sl_, e2_=e2_: e.memset(qT[:, sl_, e2_, :, :], 0.0)),
                      w=[("qT", sl_, 0, e2_), ("qT", sl_, 1, e2_)])
        for i in range(3):
            for (ub, nm) in ((ub_sc, "ubsc"), (ub_cf, "ubcf")):
                S.add("pool", (lambda e, ub=ub, i=i: e.memset(ub[:, i, :, :], 0.0)), w=[(nm, i)])

        def dump(name, ap, keys):
            if not dbg or name not in dbg_d:
                return
            d = dbg_d.pop(name)
            o = S.add("sp", lambda e: [e.dma_start(out=d.ap(), in_=ap)], r=keys, dkey=dsem(("dbg", name)), ndma=1)
            S.stores.append(o)

        def layer_prologue(l):
            def f_win(e):
                return [e.dma_start(out=win_sb[:, kc, :], in_=win_d.ap()[l, kc * P:(kc + 1) * P, :]) for kc in range(KC)]
            S.add("pool", f_win, w=[("win",)], dkey=dsem(("win",)), ndma=KC)

            def f_wout(e):
                return [e.dma_start(out=wout_sb[:, kc, :], in_=wout_d.ap()[l, kc * P:(kc + 1) * P, :]) for kc in range(KC)]
            S.add("pool", f_wout, w=[("wout",)], dkey=dsem(("wout",)), ndma=KC)
            for j in range(62):
                oc = soff["convcfT"] + l * 62 + j
                S.add("pool", (lambda e, j=j, oc=oc: e.tensor_scalar(out=dg_cf[:, j, :], in0=ident_bf[:, :],
                                                                     scalar1=small[:, oc:oc + 1], scalar2=0.5,
                                                                     op0=ALU.mult, op1=ALU.mult)),
                      r=[("ident_bf",), ("small",)], w=[("dgcf", j)])
            for j in range(6):
                o0 = soff["convscT"] + l * 6 + j
                S.add("pool", (lambda e, j=j, o0=o0: e.tensor_scalar(out=dg_sc[:, j, :], in0=ident_bf[:, :],
                                                                     scalar1=small[:, o0:o0 + 1], scalar2=None,
                                                                     op0=ALU.mult)),
                      r=[("ident_bf",), ("small",)], w=[("dgsc", j)])
            for ch in range(24):
                wi = nxt("wada", 2)
                wsl = scrA[:, wi, :].bitcast(F32).rearrange("p (k n) -> p k n", k=KC)
                src = wada_d.ap()[l, :, ch * P:(ch + 1) * P].rearrange("(k p) n -> p k n", p=P)
                S.add("sp", (lambda e, wsl=wsl, src=src: [e.dma_start(out=wsl, in_=src)]),
                      w=[("pT", wi, 0), ("pT", wi, 1)], dkey=dsem(("wada", wi)), ndma=1)
                gpa, gpk = new_gp()

                def f_mod(e, wsl=wsl, gpa=gpa):
                    ins = None
                    for kc in range(KC):
                        ins = e.matmul(gpa[:, 0:3], lhsT=wsl[:, kc, :], rhs=csT[:, kc * 3:(kc + 1) * 3],
                                       start=(kc == 0), stop=(kc == KC - 1))
                    return ins
                S.add("pe", f_mod, r=[("pT", wi, 0), ("pT", wi, 1), ("csT",)], w=[gpk])
                ob = soff["badaT"] + l * 24 + ch
                S.add("dve", (lambda e, gpa=gpa, ch=ch, ob=ob: e.tensor_scalar(
                    out=modT[:, ch, :], in0=gpa[:, 0:3], scalar1=small[:, ob:ob + 1], scalar2=None, op0=ALU.add)),
                    r=[gpk, ("small",)], w=[("modT", ch)])
            for kc in range(KC):
                og = soff["normgT"] + l * 8 + kc
                S.add("dve", (lambda e, kc=kc, og=og: e.tensor_scalar(
                    out=Gm[:, kc, :], in0=modT[:, 8 + kc, :], scalar1=1.0, scalar2=small[:, og:og + 1],
                    op0=ALU.add, op1=ALU.mult)), r=[("modT", 8 + kc), ("small",)], w=[("Gm", kc)])
            for ch in range(8):
                for m in range(3):
                    gi = nxt("grep", 1)
                    S.add("dve", (lambda e, gi=gi, ch=ch, m=m: e.tensor_scalar(
                        out=grep_t[:, gi, :], in0=ones256[:, :], scalar1=modT[:, 16 + ch, m:m + 1], scalar2=128.0,
                        op0=ALU.mult, op1=ALU.mult)), r=[("modT", 16 + ch), ("ones256",)], w=[("grep", gi)])
                    gpa, gpk = new_gp()
                    S.add("pe", (lambda e, gi=gi, gpa=gpa: e.matmul(gpa[:, 0:P], lhsT=grep_t[:, gi, :], rhs=ident_f,
                                                                    start=True, stop=True)),
                          r=[("grep", gi), ("small",)], w=[gpk])
                    S.add("act", (lambda e, gpa=gpa, ch=ch, m=m: e.activation(
                        out=gate_bc[:, m, ch * P:(ch + 1) * P], in_=gpa[:, 0:P], func=AF.Copy)),
                        r=[gpk], w=[("gate_bc", m)])

        xt_slots = {}

        def stage1a(l, b, kind, n, nblocks, fi, part="both"):
            is_ctx = kind == "c"
            m = 2 if is_ctx else b
            if l == 0:
                src_d = ctx_d if is_ctx else x_d
            else:
                src_d = ctxs_d if is_ctx else out_d
            dkey_x = ("xd", kind, b, n)
            slot = fi % 2
            hs = fi % 2
            if part in ("load", "both"):
                xts = []
                for t in range(2):
                    xi = nxt("xt", 2)
                    xts.append(xi)
                    r0 = n * BT + t * P
                    S.add("sp", (lambda e, xi=xi, r0=r0: [e.dma_start(out=xt[:, xi, :], in_=src_d.ap()[b, r0:r0 + P, :])]),
                          r=[dkey_x], w=[("xt", xi)], dkey=dsem(("xt", xi)), ndma=1)
                xt_slots[(l, fi)] = xts
                if part == "load":
                    return
            xts = xt_slots.get((l, fi), [])
            for t in range(2):
                if part == "back":
                    break
                xi = xts[t]
                S.add("act", (lambda e, xi=xi, t=t: e.activation(out=xn[:, t, :], in_=xt[:, xi, :], func=AF.Square,
                                                                 accum_out=ssb[:, slot, t:t + 1])),
                      r=[("xt", xi)], w=[("xn", t), ("ssb", slot, t)])
            if part != "back":
              S.add("dve", lambda e: e.tensor_scalar(out=rsb[:, slot, 0:2], in0=ssb[:, slot, 0:2], scalar1=1.0 / D,
                                                   scalar2=NORM_EPS, op0=ALU.mult, op1=ALU.add),
                  r=[("ssb", slot, 0), ("ssb", slot, 1)], w=[("rsb", slot)])
              S.add("dve", lambda e: e.reciprocal(out=rsb[:, slot, 0:2], in_=rsb[:, slot, 0:2]), r=[("rsb", slot)], w=[("rsb", slot)])
              S.add("act", lambda e: e.activation(out=rsb[:, slot, 0:2], in_=rsb[:, slot, 0:2], func=AF.Sqrt),
                    r=[("rsb", slot)], w=[("rsb", slot)])
              for t in range(2):
                xi = xts[t]
                S.add("pool", (lambda e, xi=xi, t=t: e.tensor_scalar(out=xn[:, t, :], in0=xt[:, xi, :],
                                                                     scalar1=rsb[:, slot, t:t + 1], scalar2=None, op0=ALU.mult)),
                      r=[("xt", xi), ("rsb", slot)], w=[("xn", t)])
            for t in range(2):
                if part == "front":
                    break
                gpa, gpk = new_gp()
                tp = gpa.bitcast(BF16)

                def f_tr(e, t=t, tp=tp):
                    ins = None
                    for kc in range(KC):
                        ins = e.transpose(out=tp[:, kc * P:(kc + 1) * P], in_=xn[:, t, kc * P:(kc + 1) * P],
                                          identity=ident_bf[:, :])
                    return ins
                S.add("pe", f_tr, r=[("xn", t), ("ident_bf",)], w=[gpk])
                hcf, hck = new_tmp()
                hc = hcf.bitcast(BF16)
                S.add("dve", (lambda e, tp=tp, hc=hc: e.tensor_copy(out=hc, in_=tp)), r=[gpk], w=[hck])
                for kc in range(KC):
                    if kc % 2 == 0:
                        S.add("pool", (lambda e, kc=kc, t=t, hc=hc: e.tensor_scalar(
                            out=hT[:, hs, kc, t * P:(t + 1) * P], in0=hc[:, kc * P:(kc + 1) * P],
                            scalar1=Gm[:, kc, m:m + 1], scalar2=modT[:, kc, m:m + 1], op0=ALU.mult, op1=ALU.add)),
                            r=[hck, ("Gm", kc), ("modT", kc)], w=[("hT", hs, t, kc)])
                    else:
                        S.add("dve", (lambda e, kc=kc, t=t, hc=hc: e.tensor_scalar(
                            out=hT[:, hs, kc, t * P:(t + 1) * P], in0=hc[:, kc * P:(kc + 1) * P],
                            scalar1=Gm[:, kc, m:m + 1], scalar2=modT[:, kc, m:m + 1], op0=ALU.mult, op1=ALU.add)),
                            r=[hck, ("Gm", kc), ("modT", kc)], w=[("hT", hs, t, kc)])

        def stage1b(l, b, kind, n, nblocks, fi):
            is_ctx = kind == "c"
            slot = fi % 2
            hs = fi % 2
            us = fi % 3
            hkeys = [("hT", hs, t, kc) for t in range(2) for kc in range(KC)]
            if dbg and l == dbg.get("_l", 0) and b == 0 and n == dbg.get("_n", 0) and (is_ctx == dbg.get("_ctx", False)):
                pass

            def proj_pair(fc0):
                gpa, gpk = new_gp()

                def f(e, gpa=gpa):
                    ins = None
                    for c in range(2):
                        fc = fc0 + c
                        for kc in range(KC):
                            ins = e.matmul(gpa[:, c * BT:(c + 1) * BT], lhsT=win_sb[:, kc, fc * P:(fc + 1) * P],
                                           rhs=hT[:, hs, kc, :], start=(kc == 0), stop=(kc == KC - 1))
                    return ins
                S.add("pe", f, r=hkeys + [("win",)], w=[gpk])
                return gpa, gpk

            def v3(ap2):
                return ap2.rearrange("p (c t) -> p c t", c=2)

            gpa, gpk = proj_pair(0)
            zh, zhk = new_tmp()
            S.add("act", (lambda e, gpa=gpa, zh=zh: e.activation(out=zh, in_=gpa, func=AF.Copy)), r=[gpk], w=[zhk])
            gpa, gpk = proj_pair(4)
            S.add("dve", (lambda e, gpa=gpa, zh=zh: e.tensor_tensor(out=ub_sc[:, us, :, 16:272], in0=v3(gpa), in1=v3(zh),
                                                                   op=ALU.mult)), r=[gpk, zhk], w=[("ubsc", us)])
            gpa, gpk = proj_pair(6)
            th, thk = new_tmp()
            S.add("act", (lambda e, gpa=gpa, th=th: e.activation(out=th, in_=gpa, func=AF.Tanh, scale=0.5)), r=[gpk], w=[thk])
            S.add("dve", (lambda e, gpa=gpa, th=th: e.scalar_tensor_tensor(out=th, in0=th, scalar=1.0, in1=gpa,
                                                                          op0=ALU.add, op1=ALU.mult)), r=[gpk, thk], w=[thk])
            gpa, gpk = proj_pair(2)
            S.add("dve", (lambda e, gpa=gpa, th=th: e.tensor_tensor(out=bg[:, slot, :, :], in0=v3(gpa), in1=v3(th),
                                                                   op=ALU.mult)), r=[gpk, thk], w=[("bg", slot)])
            gpa, gpk = proj_pair(26)
            th, thk = new_tmp()
            S.add("act", (lambda e, gpa=gpa, th=th: e.activation(out=th, in_=gpa, func=AF.Tanh, scale=0.5)), r=[gpk], w=[thk])
            gpa, gpk = proj_pair(24)
            S.add("dve", (lambda e, gpa=gpa, th=th: e.scalar_tensor_tensor(
                out=ub_cf[:, us, :, 16:272], in0=v3(th), scalar=1.0, in1=v3(gpa), op0=ALU.add, op1=ALU.mult)),
                r=[gpk, thk], w=[("ubcf", us)])
            gpa, gpk = proj_pair(28)
            th, thk = new_tmp()
            S.add("act", (lambda e, gpa=gpa, th=th: e.activation(out=th, in_=gpa, func=AF.Tanh, scale=0.5)), r=[gpk], w=[thk])
            S.add("dve", (lambda e, gpa=gpa, th=th: e.scalar_tensor_tensor(
                out=cg[:, slot, :, :], in0=v3(th), scalar=1.0, in1=v3(gpa), op0=ALU.add, op1=ALU.mult)),
                r=[gpk, thk], w=[("cg", slot)])
            for (ub, nm) in ((ub_sc, "ubsc"), (ub_cf, "ubcf")):
                if n == 0:
                    S.add("pool", (lambda e, ub=ub: e.memset(ub[:, us, :, 0:16], 0.0)), w=[(nm, us)])
                else:
                    pu = (fi - 1) % 3
                    S.add("pool", (lambda e, ub=ub, pu=pu: e.tensor_copy(out=ub[:, pu, :, 272:288], in_=ub[:, us, :, 16:32])),
                          r=[(nm, us)], w=[(nm, pu)])
                if n == nblocks - 1:
                    S.add("pool", (lambda e, ub=ub: e.memset(ub[:, us, :, 272:288], 0.0)), w=[(nm, us)])
                else:
                    nu = (fi + 1) % 3
                    S.add("pool", (lambda e, ub=ub, nu=nu: e.tensor_copy(out=ub[:, nu, :, 0:16], in_=ub[:, us, :, 256:272])),
                          r=[(nm, us)], w=[(nm, nu)])
            for i in range(2):
                gpa, gpk = proj_pair(8 + 2 * i)
                S.add("act", (lambda e, gpa=gpa, i=i: e.activation(out=qT[0:64, slot, 0, 2 * i:2 * i + 2, :], in_=v3(gpa)[0:64],
                                                                   func=AF.Identity, scale=0.125)), r=[gpk], w=[("qT", slot, i, 0)])
                S.add("dve", (lambda e, gpa=gpa, i=i: e.tensor_scalar(out=qT[64:128, slot, 1, 2 * i:2 * i + 2, :], in0=v3(gpa)[64:128],
                                                                      scalar1=0.125, scalar2=None, op0=ALU.mult)),
                      r=[gpk], w=[("qT", slot, i, 1)])
            for i in range(2):
                gpa, gpk = proj_pair(12 + 2 * i)
                if is_ctx:
                    dst, dk = kTc[:, 2 * i:2 * i + 2, :], ("kTc", i)
                else:
                    dst, dk = kT[:, 2 * i:2 * i + 2, (n % 4) * BT:(n % 4 + 1) * BT], ("kT", n % 4, i)
                S.add("dve", (lambda e, gpa=gpa, dst=dst: e.tensor_copy(out=dst, in_=v3(gpa))), r=[gpk], w=[dk])
            for t in range(2):
                gpa, gpk = new_gp()

                def f(e, gpa=gpa, t=t):
                    ins = None
                    for kc in range(KC):
                        ins = e.matmul(gpa, lhsT=hT[:, hs, kc, t * P:(t + 1) * P], rhs=win_sb[:, kc, 2048:2560],
                                       start=(kc == 0), stop=(kc == KC - 1))
                    return ins
                S.add("pe", f, r=hkeys + [("win",)], w=[gpk])
                if is_ctx:
                    dst, dk = Vc[:, t, :], ("Vc", t)
                else:
                    dst, dk = Vr[:, 2 * (n % 4) + t, :], ("V", n % 4, t)
                if t == 0:
                    S.add("act", (lambda e, gpa=gpa, dst=dst: e.activation(out=dst, in_=gpa, func=AF.Copy)), r=[gpk], w=[dk])
                else:
                    S.add("dve", (lambda e, gpa=gpa, dst=dst: e.tensor_copy(out=dst, in_=gpa)), r=[gpk], w=[dk])
            for i in range(2):
                gpa, gpk = proj_pair(20 + 2 * i)
                th, thk = new_tmp()
                S.add("act", (lambda e, gpa=gpa, th=th: e.activation(out=th, in_=gpa, func=AF.Tanh, scale=0.5)), r=[gpk], w=[thk])
                S.add("dve", (lambda e, gpa=gpa, th=th, i=i: e.scalar_tensor_tensor(
                    out=ga[:, slot, 2 * i:2 * i + 2, :], in0=v3(th), scalar=1.0, in1=v3(gpa), op0=ALU.add, op1=ALU.mult)),
                    r=[gpk, thk], w=[("ga", slot, i)])

        def stage2(l, b, kind, n, nblocks, fi, hook=None, hook_front=None):
            is_ctx = kind == "c"
            last_layer = l == L - 1
            m = 2 if is_ctx else b
            if l == 0:
                src_d = ctx_d if is_ctx else x_d
            else:
                src_d = ctxs_d if is_ctx else out_d
            dst_d = ctxs_d if is_ctx else out_d
            dkey_x = ("xd", kind, b, n)
            slot = fi % 2
            us = fi % 3
            if is_ctx:
                wch = []
                ty = None
            elif n == 0:
                wch = [(c, (6 - 2 * c) * 64) for c in range(4)]
                ty = 0
            elif n == nblocks - 1:
                wch = [(2 * nblocks - 4 + c, (10 - 2 * c) * 64) for c in range(4)]
                ty = 0
            else:
                wch = [(2 * n - 2 + d, (10 - 2 * d) * 64) for d in range(6)]
                ty = 1
            chunks = [("w", ch, off) for (ch, off) in wch] + [("c", 0, None), ("c", 1, None)]
            groups = [chunks[i:i + 4] for i in range(0, len(chunks), 4)]
            ll_slot = {}
            xr_slots = {}

            def load_xr(t_):
                for hf_ in range(2):
                    xi = nxt("xr", 2)
                    xr_slots[(t_, hf_)] = xi
                    rr0 = n * BT + t_ * P
                    S.add("sp", (lambda e, xi=xi, rr0=rr0, hf_=hf_: [e.dma_start(
                        out=xr[:, xi, :], in_=src_d.ap()[b, rr0:rr0 + P, hf_ * 512:(hf_ + 1) * 512])]),
                        r=[dkey_x], w=[("xr", xi)], dkey=dsem(("xr", xi)), ndma=1)
            load_xr(0)

            def issue_LL(h):
                if is_ctx or h >= 8:
                    return
                li = h % 2
                ll_slot[h] = li
                S.add("pool", (lambda e, li=li, h=h: [e.dma_start(out=LLs[:, li, :], in_=LL_d.ap()[l, h, ty, :, :])]),
                      w=[("LL", li)], dkey=dsem(("LL", li)), ndma=1)

            hstate = {}

            def attn_qk(h):
                hp, e_ = h // 2, h % 2
                r0, r1 = 64 * e_, 64 * e_ + 64
                li = ll_slot.get(h, 0)
                pi = nxt("pT", 2)
                pT = scrA[:, pi, :]
                ci = 0
                ginfo = []
                for gi, g in enumerate(groups):
                    spa, spk = new_sps()
                    spf = spa.rearrange("p a b -> p (a b)")
                    rk = [("qT", slot, hp // 2, e_)]
                    for (kind_c, ch, off) in g:
                        if kind_c == "w":
                            rk += [("kT", (ch // 2) % 4, hp // 2), ("LL", li), ("ident_bf",)]
                        else:
                            rk += [("kTc", hp // 2)]

                    def f_qk(e, g=g, spf=spf, hp=hp, r0=r0, r1=r1, li=li, e_=e_):
                        ins = None
                        for j, (kind_c, ch, off) in enumerate(g):
                            o_ = spf[:, j * BT:(j + 1) * BT]
                            if kind_c == "w":
                                rc = (ch % 8) * P
                                e.matmul(o_, lhsT=kT[:, hp, rc:rc + P], rhs=qT[:, slot, e_, hp, :],
                                         start=True, stop=False)
                                ins = e.matmul(o_, lhsT=ident_bf[:, :], rhs=LLs[:, li, off:off + BT],
                                               start=False, stop=True)
                            else:
                                ins = e.matmul(o_, lhsT=kTc[:, hp, ch * P:(ch + 1) * P], rhs=qT[:, slot, e_, hp, :],
                                               start=True, stop=True)
                        return ins
                    S.add("pe", f_qk, r=rk, w=[spk])
                    ng = len(g)
                    S.add("act", (lambda e, spf=spf, pT=pT, ci=ci, ng=ng: e.activation(
                        out=pT[:, ci * BT:(ci + ng) * BT], in_=spf[:, 0:ng * BT], func=AF.Exp)),
                        r=[spk], w=[("pT", pi, gi)])
                    gl = []
                    for (kind_c, ch, off) in g:
                        gl.append((kind_c, ch, ci))
                        ci += 1
                    ginfo.append(gl)
                issue_LL(h + 2)
                hstate[h] = (pi, pT, ginfo, ci)

            def attn_pv(h):
                hp, e_ = h // 2, h % 2
                r0, r1 = 64 * e_, 64 * e_ + 64
                pi, pT, ginfo, ci = hstate.pop(h)
                gpa, gpk = new_gp()
                nch = ci
                for gi, gl in enumerate(ginfo):
                    rk = [("pT", pi, gi), ("ones_bf",)]
                    for (kind_c, ch, cpos) in gl:
                        rk.append(("V", (ch // 2) % 4, ch % 2) if kind_c == "w" else ("Vc", ch))

                    def f_pv(e, gl=gl, gpa=gpa, pT=pT, hp=hp, nch=nch, last=(gi == len(ginfo) - 1)):
                        ins = None
                        for (kind_c, ch, cpos) in gl:
                            lw = Vr[:, ch % 8, hp * P:(hp + 1) * P] if kind_c == "w" else Vc[:, ch, hp * P:(hp + 1) * P]
                            ins = e.matmul(gpa[:, 0:BT], lhsT=lw, rhs=pT[:, cpos * BT:(cpos + 1) * BT],
                                           start=(cpos == 0), stop=(cpos == nch - 1))
                        if last:
                            for cpos in range(nch):
                                ins = e.matmul(gpa[:, BT:2 * BT], lhsT=ones_bf[:, :], rhs=pT[:, cpos * BT:(cpos + 1) * BT],
                                               start=(cpos == 0), stop=(cpos == nch - 1))
                        return ins
                    S.add("pe", f_pv, r=rk + ([gpk] if gi > 0 else []) + ([("pT", pi, j) for j in range(len(ginfo))] if gi == len(ginfo) - 1 else []),
                          w=[gpk])
                t1, t1k = new_tmp()
                S.add("dve", (lambda e, gpa=gpa, t1=t1, r0=r0, r1=r1: e.reciprocal(out=t1[r0:r1, 0:BT], in_=gpa[r0:r1, BT:2 * BT])),
                      r=[gpk], w=[t1k])
                S.add("dve", (lambda e, gpa=gpa, t1=t1, r0=r0, r1=r1: e.tensor_tensor(
                    out=t1[r0:r1, BT:2 * BT], in0=gpa[r0:r1, 0:BT], in1=t1[r0:r1, 0:BT], op=ALU.mult)),
                    r=[gpk, t1k], w=[t1k])
                S.add("pool", (lambda e, t1=t1, r0=r0, r1=r1, hp=hp: e.tensor_tensor(
                    out=yT[r0:r1, 2 + hp, :], in0=t1[r0:r1, BT:2 * BT], in1=ga[r0:r1, slot, hp, :], op=ALU.mult)),
                    r=[t1k, ("ga", slot, hp // 2)], w=[("yT", 2 + hp, e_)])

            issue_LL(0)
            issue_LL(1)
            gpc, gpck = new_gp()

            def f_cf(e, gpa=gpc):
                ins = None
                for c in range(2):
                    for k in range(31):
                        ins = e.matmul(gpa[:, c * BT:(c + 1) * BT], lhsT=dg_cf[:, c * 31 + k, :],
                                       rhs=ub_cf[:, us, c, 1 + k:1 + k + BT], start=(k == 0), stop=(k == 30))
                return ins
            S.add("pe", f_cf, r=[("ubcf", us)] + [("dgcf", j) for j in range(62)], w=[gpck])
            vcf, vcfk = lnt[:, 0, :], ("lnt", 0)
            vsq, vsqk = lnt[:, 1, :], ("lnt", 1)
            for c in range(2):
                ob = soff["cfbT"] + l * 2 + c
                S.add("act", (lambda e, gpa=gpc, vcf=vcf, c=c, ob=ob: e.activation(
                    out=vcf[:, c * BT:(c + 1) * BT], in_=gpa[:, c * BT:(c + 1) * BT], func=AF.Identity,
                    bias=small[:, ob:ob + 1])), r=[gpck, ("small",)], w=[vcfk])
                S.add("act", (lambda e, gpa=gpc, vsq=vsq, c=c, ob=ob: e.activation(
                    out=vsq[:, c * BT:(c + 1) * BT], in_=gpa[:, c * BT:(c + 1) * BT], func=AF.Square,
                    bias=small[:, ob:ob + 1])), r=[gpck, ("small",)], w=[vsqk])
            gpa, gpk = new_gp()

            def f_sc(e, gpa=gpa):
                ins = None
                for c in range(2):
                    for k in range(3):
                        ins = e.matmul(gpa[:, c * BT:(c + 1) * BT], lhsT=dg_sc[:, c * 3 + k, :],
                                       rhs=ub_sc[:, us, c, 15 + k:15 + k + BT], start=(k == 0), stop=(k == 2))
                return ins
            S.add("pe", f_sc, r=[("ubsc", us)] + [("dgsc", j) for j in range(6)], w=[gpk])
            S.add("dve", (lambda e, gpa=gpa: e.tensor_tensor(out=yT[:, 0:2, :], in0=gpa.rearrange("p (c t) -> p c t", c=2),
                                                            in1=bg[:, slot, :, :], op=ALU.mult)),
                  r=[gpk, ("bg", slot)], w=[("yT", 0, 0), ("yT", 1, 0)])
            attn_qk(0)
            attn_qk(1)
            attn_pv(0)
            attn_qk(2)
            attn_pv(1)
            gps, gpsk = new_gp()

            def f_st(e, gps=gps, vcf=vcf, vsq=vsq):
                e.matmul(gps[:, 0:BT], lhsT=ones256[:, :], rhs=vcf[:, 0:BT], start=True, stop=False)
                e.matmul(gps[:, 0:BT], lhsT=ones256[:, :], rhs=vcf[:, BT:2 * BT], start=False, stop=True)
                e.matmul(gps[:, BT:2 * BT], lhsT=ones256[:, :], rhs=vsq[:, 0:BT], start=True, stop=False)
                return e.matmul(gps[:, BT:2 * BT], lhsT=ones256[:, :], rhs=vsq[:, BT:2 * BT], start=False, stop=True)
            S.add("pe", f_st, r=[vcfk, vsqk, ("ones256",)], w=[gpsk])
            st, stk = lnt[:, 1, :], ("lnt", 1)
            S.add("act", (lambda e, gps=gps, st=st: e.activation(out=st[:, 0:BT], in_=gps[:, 0:BT], func=AF.Square)),
                  r=[gpsk], w=[stk])
            S.add("dve", (lambda e, gps=gps, st=st: e.tensor_tensor(out=st[:, 0:BT], in0=gps[:, BT:2 * BT], in1=st[:, 0:BT],
                                                                   op=ALU.subtract)), r=[gpsk, stk], w=[stk])
            S.add("dve", (lambda e, st=st: e.tensor_scalar(out=st[:, 0:BT], in0=st[:, 0:BT], scalar1=LN_EPS, scalar2=None,
                                                          op0=ALU.add)), r=[stk], w=[stk])
            S.add("dve", (lambda e, st=st: e.reciprocal(out=st[:, 0:BT], in_=st[:, 0:BT])), r=[stk], w=[stk])
            S.add("act", (lambda e, gps=gps, st=st: e.activation(out=st[:, BT:2 * BT], in_=gps[:, 0:BT], func=AF.Copy)),
                  r=[gpsk, stk], w=[stk])
            attn_qk(3)
            attn_pv(2)
            attn_qk(4)
            attn_pv(3)
            if hook_front is not None:
                hook_front()
            S.add("act", (lambda e, st=st: e.activation(out=st[:, 0:BT], in_=st[:, 0:BT], func=AF.Sqrt)), r=[stk], w=[stk])
            for c in range(2):
                S.add("dve", (lambda e, vcf=vcf, st=st, c=c: e.tensor_tensor(
                    out=vcf[:, c * BT:(c + 1) * BT], in0=vcf[:, c * BT:(c + 1) * BT], in1=st[:, BT:2 * BT], op=ALU.subtract)),
                    r=[vcfk, stk], w=[vcfk])
                S.add("pool", (lambda e, vcf=vcf, st=st, c=c: e.tensor_tensor(
                    out=vcf[:, c * BT:(c + 1) * BT], in0=vcf[:, c * BT:(c + 1) * BT], in1=st[:, 0:BT], op=ALU.mult)),
                    r=[vcfk, stk], w=[vcfk])
                S.add("dve", (lambda e, vcf=vcf, c=c: e.tensor_scalar(
                    out=vcf[:, c * BT:(c + 1) * BT], in0=vcf[:, c * BT:(c + 1) * BT],
                    scalar1=hln[:, l * 2 + c:l * 2 + c + 1], scalar2=hln[:, L * 2 + l * 2 + c:L * 2 + l * 2 + c + 1],
                    op0=ALU.mult, op1=ALU.add)), r=[vcfk, ("hln",)], w=[vcfk])
            attn_qk(5)
            attn_pv(4)
            attn_qk(6)
            attn_pv(5)
            S.add("act", (lambda e, vcf=vcf, vsq=vsq: e.activation(out=vsq, in_=vcf, func=AF.Tanh)), r=[vcfk, vsqk], w=[vsqk])
            S.add("dve", (lambda e, vcf=vcf, vsq=vsq: e.scalar_tensor_tensor(out=vsq, in0=vsq, scalar=1.0, in1=vcf,
                                                                            op0=ALU.add, op1=ALU.mult)),
                  r=[vcfk, vsqk], w=[vsqk])
            S.add("pool", (lambda e, vsq=vsq: e.tensor_tensor(out=yT[:, 6:8, :], in0=vsq.rearrange("p (c t) -> p c t", c=2),
                                                             in1=cg[:, slot, :, :], op=ALU.mult)),
                  r=[vsqk, ("cg", slot)], w=[("yT", 6, 0), ("yT", 7, 0)])
            if hook is not None:
                hook()
            attn_qk(7)
            attn_pv(6)
            attn_pv(7)
            ykeys = [("yT", 0, 0), ("yT", 1, 0), ("yT", 6, 0), ("yT", 7, 0)] + \
                    [("yT", 2 + hp, e_) for hp in range(4) for e_ in range(2)]
            if dbg and "yT" in dbg_d and l == dbg.get("_l", 0) and b == 0 and n == dbg.get("_n", 0) and (is_ctx == dbg.get("_ctx", False)):
                pass
            fin_norm = last_layer and not is_ctx
            pend_all = {}
            for t in range(2):
                pend = []
                for hf in range(2):
                    gpa, gpk = new_gp()

                    def f_oa(e, gpa=gpa, t=t, hf=hf):
                        ins = None
                        for j, kc in enumerate((0, 1, 6, 7, 2, 3, 4)):
                            ins = e.matmul(gpa, lhsT=yT[:, kc, t * P:(t + 1) * P], rhs=wout_sb[:, kc, hf * 512:(hf + 1) * 512],
                                           start=(j == 0), stop=False)
                        return ins

                    def f_ob(e, gpa=gpa, t=t, hf=hf):
                        return e.matmul(gpa, lhsT=yT[:, 5, t * P:(t + 1) * P], rhs=wout_sb[:, 5, hf * 512:(hf + 1) * 512],
                                        start=False, stop=True)
                    ykA = [k for k in ykeys if k[1] != 5]
                    ykB = [k for k in ykeys if k[1] == 5]
                    S.add("pe", f_oa, r=ykA + [("wout",)], w=[gpk])
                    pend.append((f_ob, ykB, gpa, gpk, hf))
                pend_all[t] = pend
            for t in range(2):
                r0 = n * BT + t * P
                xnews = []
                pend = pend_all[t]
                for (f_ob, ykB, gpa, gpk, hf) in pend:
                    S.add("pe", f_ob, r=ykB + [("wout",), gpk], w=[gpk])
                    xi = xr_slots[(t, hf)]
                    xw, xwk = new_tmp()
                    S.add("dve", (lambda e, gpa=gpa, xw=xw, hf=hf: e.tensor_tensor(
                        out=xw, in0=gpa, in1=gate_bc[:, m, hf * 512:(hf + 1) * 512], op=ALU.mult)),
                        r=[gpk, ("gate_bc", m)], w=[xwk])
                    S.add("pool", (lambda e, xw=xw, xi=xi: e.tensor_tensor(out=xw, in0=xw, in1=xr[:, xi, :], op=ALU.add)),
                          r=[xwk, ("xr", xi)], w=[xwk])
                    xnews.append((xw, xwk, hf))
                if t == 0:
                    load_xr(1)
                if fin_norm:
                    fs = (n * 2 + t) % 2
                    jk, jkk = new_tmp()
                    for (xw, xwk, hf) in xnews:
                        S.add("act", (lambda e, xw=xw, jk=jk, hf=hf, fs=fs: e.activation(
                            out=jk, in_=xw, func=AF.Square, accum_out=ssf[:, fs, hf:hf + 1])),
                            r=[xwk], w=[jkk, ("ssf", fs, hf)])
                    S.add("dve", (lambda e, fs=fs: e.tensor_tensor(out=ssf[:, fs, 2:3], in0=ssf[:, fs, 0:1], in1=ssf[:, fs, 1:2],
                                                                  op=ALU.add)), r=[("ssf", fs, 0), ("ssf", fs, 1)], w=[("ssf", fs, 2)])
                    S.add("dve", (lambda e, fs=fs: e.tensor_scalar(out=ssf[:, fs, 2:3], in0=ssf[:, fs, 2:3], scalar1=1.0 / D,
                                                                  scalar2=NORM_EPS, op0=ALU.mult, op1=ALU.add)),
                          r=[("ssf", fs, 2)], w=[("ssf", fs, 2)])
                    S.add("dve", (lambda e, fs=fs: e.reciprocal(out=ssf[:, fs, 2:3], in_=ssf[:, fs, 2:3])),
                          r=[("ssf", fs, 2)], w=[("ssf", fs, 2)])
                    S.add("act", (lambda e, fs=fs: e.activation(out=ssf[:, fs, 3:4], in_=ssf[:, fs, 2:3], func=AF.Sqrt)),
                          r=[("ssf", fs, 2)], w=[("ssf", fs, 3)])
                    for (xw, xwk, hf) in xnews:
                        S.add("dve", (lambda e, xw=xw, hf=hf, fs=fs: e.scalar_tensor_tensor(
                            out=xw, in0=xw, scalar=ssf[:, fs, 3:4], in1=finalg_bc[:, hf * 512:(hf + 1) * 512],
                            op0=ALU.mult, op1=ALU.mult)), r=[xwk, ("ssf", fs, 3), ("finalg",)], w=[xwk])
                for (xw, xwk, hf) in xnews:
                    o = S.add("sp", (lambda e, xw=xw, r0=r0, hf=hf: [e.dma_start(
                        out=dst_d.ap()[b, r0:r0 + P, hf * 512:(hf + 1) * 512], in_=xw)]),
                        r=[xwk], w=[dkey_x], dkey=dsem(("st", xwk[1])), ndma=1)
                    S.stores.append(o)

        for l in range(L):
            layer_prologue(l)
            blks = []
            for b in range(2):
                blks.append((b, "c", 0, 1))
                for n in range(NBL):
                    blks.append((b, "x", n, NBL))
            M = len(blks)
            stage1a(l, *blks[0], 0)
            stage1a(l, *blks[1], 1)
            stage1b(l, *blks[0], 0)
            stage1a(l, *blks[2], 2, part="load")
            for i in range(M):
                hook = None
                hook_front = None
                if i + 2 < M:
                    def hook_front(i=i):
                        stage1a(l, *blks[i + 2], i + 2, part="front")
                        if i + 3 < M:
                            stage1a(l, *blks[i + 3], i + 3, part="load")
                    hook = (lambda i=i: stage1a(l, *blks[i + 2], i + 2, part="back"))
                late = (i + 1 < M and blks[i + 1][1] == "c")
                if i + 1 < M and not late:
                    stage1b(l, *blks[i + 1], i + 1)
                if not (blks[i][1] == "c" and l == L - 1):
                    stage2(l, *blks[i], i, hook=hook, hook_front=hook_front)
                else:
                    if hook_front is not None:
                        hook_front()
                    if hook is not None:
                        hook()
                if late:
                    stage1b(l, *blks[i + 1], i + 1)
        fin = S.add("sp", None)
        fin.deps = list(S.stores)
        S.finalize()

        block = es.enter_context(nc.Block())

        @block.tensor
        def _(e):
            S.emit_engine("pe", e, sems, dsems)

        @block.scalar
        def _(e):
            S.emit_engine("act", e, sems, dsems)

        @block.vector
        def _(e):
            S.emit_engine("dve", e, sems, dsems)

        @block.gpsimd
        def _(e):
            S.emit_engine("pool", e, sems, dsems)

        @block.sync
        def _(e):
            S.emit_engine("sp", e, sems, dsems)

    return nc, S


def build_LL(rpb):
    L = rpb.shape[0]
    kc = np.arange(64)[:, None]
    qc = np.arange(64)[None, :]
    cs = np.clip(qc - 8, 0, 48)
    colvalid = (kc >= cs) & (kc < cs + 16)
    cidx = np.clip(kc - qc + 15, 0, 30)
    LL = np.full((L, 8, 2, 2, 64, 14, 64), NEG, np.float32)
    for krl in range(2):
        for si, s in enumerate(range(-6, 8)):
            dr = krl - s
            if abs(dr) > 7:
                continue
            blk = rpb[:, :, dr + 7, :][:, :, cidx]
            blk = np.where(colvalid, blk, np.float32(NEG)).astype(np.float32)
            LL[:, :, 0, krl, :, si, :] = blk
            if -4 <= dr <= 3:
                LL[:, :, 1, krl, :, si, :] = blk
    return np.ascontiguousarray(LL.reshape(L, 8, 2, P, LLW))


def fm(v, nchunk):
    v = np.asarray(v, np.float32)
    lead = v.shape[:-1]
    v = v.reshape(lead + (nchunk, P))
    return np.moveaxis(v, -1, 0)


def make_in_maps(inputs, L, ncores):
    x = np.asarray(inputs["x"], np.float32)
    c = np.asarray(inputs["c"], np.float32)
    ctx = np.asarray(inputs["ctx"], np.float32)
    c_ctx = np.asarray(inputs["c_ctx"], np.float32)
    soff, NSM = small_layout(L)
    LLb = build_LL(np.asarray(inputs["rpb"], np.float32))
    w_ada = np.ascontiguousarray(inputs["w_ada"], dtype=np.float32)
    w_in = np.ascontiguousarray(inputs["w_in"], dtype=np.float32)
    w_out = np.ascontiguousarray(inputs["w_out"], dtype=np.float32)
    fg = np.ascontiguousarray(np.asarray(inputs["final_g"], np.float32).reshape(1, D))
    base = np.zeros((P, NSM), np.float32)

    def put(name, arr):
        arr = np.asarray(arr, np.float32).reshape(P, -1)
        base[:, soff[name]:soff[name] + arr.shape[1]] = arr
    put("normgT", fm(inputs["norm_g"], 8))
    put("badaT", fm(inputs["b_ada"], 24))
    put("convscT", np.transpose(fm(inputs["conv_sc"], 2), (0, 1, 3, 2)))
    put("convcfT", np.transpose(fm(inputs["conv_cf"], 2), (0, 1, 3, 2)))
    put("cfbT", fm(inputs["conv_cf_b"], 2))
    put("lngT", fm(inputs["ln_cf_g"], 2))
    put("lnbT", fm(inputs["ln_cf_b"], 2))
    put("ident", np.eye(P, dtype=np.float32))
    in_maps = []
    for i in range(ncores):
        sm = base.copy()
        cv = np.stack([c[2 * i], c[2 * i + 1], c_ctx], axis=0)
        cT = np.transpose(cv.reshape(3, 8, P), (2, 1, 0))
        sm[:, soff["cT"]:soff["cT"] + 24] = cT.reshape(P, 24)
        in_maps.append({
            "x": np.ascontiguousarray(x[2 * i:2 * i + 2]),
            "ctx": np.ascontiguousarray(ctx[2 * i:2 * i + 2]),
            "small": sm,
            "w_ada": w_ada, "w_in": w_in, "w_out": w_out, "LLb": LLb, "final_g": fg,
        })
    return in_maps


_CACHE = {}


def kernel(**inputs):
    L = int(np.asarray(inputs["w_in"]).shape[0])
    seq = int(np.asarray(inputs["x"]).shape[1])
    nb = int(np.asarray(inputs["x"]).shape[0])
    ncores = nb // 2
    key = (seq, L)
    if key not in _CACHE:
        _CACHE[key] = build_program(seq // GW, L)[0]
    nc = _CACHE[key]
    in_maps = make_in_maps(inputs, L, ncores)
    res = run_bass_kernel_spmd(nc, in_maps, core_ids=list(range(ncores)))
    return np.concatenate([np.asarray(r["out"]) for r in res.results], axis=0).astype(np.float32)
```

```python
import contextlib
import numpy as np
import concourse.bass as bass
import concourse.mybir as mybir
from concourse.bass_utils import run_bass_kernel_spmd

F32 = mybir.dt.float32
BF16 = mybir.dt.bfloat16
ALU = mybir.AluOpType
AF = mybir.ActivationFunctionType

P = 128
D = 1024
KC = 8
DP = 3840
GW = 64
BT = 256
CTXL = 256
NEG = -30000.0
NORM_EPS = 1e-6
LN_EPS = 1e-5
LLW = 14 * 64


class _Op:
    __slots__ = ("eng", "fn", "deps", "is_dma", "dkey", "ndma", "signal", "token_val")


class Sched:
    ENGS = ("pe", "act", "dve", "pool", "sp")

    def __init__(self, same=("act", "dve", "pool")):
        self.ops = {e: [] for e in self.ENGS}
        self.lw = {}
        self.rd = {}
        self.same = set(same)
        self.dma_count = {}
        self.stores = []

    def add(self, eng, fn, r=(), w=(), dkey=None, ndma=0):
        import os as _os
        self.nadd = getattr(self, "nadd", 0) + 1
        if self.nadd > int(_os.environ.get("KMAXOPS", "100000000")):
            o = _Op()
            o.is_dma = dkey is not None
            o.deps = []
            return o
        o = _Op()
        o.eng = eng
        o.fn = fn
        o.dkey = dkey
        o.ndma = ndma
        o.is_dma = dkey is not None
        o.signal = False
        o.token_val = 0
        deps = {}
        for k in r:
            p = self.lw.get(k)
            if p is not None:
                deps[id(p)] = p
        for k in w:
            p = self.lw.get(k)
            if p is not None:
                deps[id(p)] = p
            rdk = self.rd.get(k)
            if rdk is not None:
                for q in rdk[0].values():
                    deps[id(q)] = q
                for q in rdk[1]:
                    deps[id(q)] = q
        deps.pop(id(o), None)
        o.deps = list(deps.values())
        for k in r:
            rdk = self.rd.get(k)
            if rdk is None:
                rdk = self.rd[k] = ({}, [])
            if o.is_dma:
                rdk[1].append(o)
            else:
                rdk[0][eng] = o
        for k in w:
            self.lw[k] = o
            self.rd[k] = ({}, [])
        if o.is_dma:
            c = self.dma_count.get(dkey, 0) + 16 * ndma
            self.dma_count[dkey] = c
            o.token_val = c
        self.ops[eng].append(o)
        return o

    def finalize(self):
        for e, lst in self.ops.items():
            for o in lst:
                for d in o.deps:
                    if d.is_dma:
                        continue
                    if d.eng != o.eng or (o.eng in self.same) or o.is_dma:
                        d.signal = True
        for e, lst in self.ops.items():
            c = 0
            for o in lst:
                if not o.is_dma and o.signal:
                    c += 1
                    o.token_val = c

    def emit_engine(self, engname, eng, sems, dsems):
        waited = {}
        for o in self.ops[engname]:
            for d in o.deps:
                if d.is_dma:
                    sk, sem, val = ("d", d.dkey), dsems[d.dkey], d.token_val
                else:
                    if d.eng == engname and engname not in self.same and not o.is_dma:
                        continue
                    sk, sem, val = ("e", d.eng), sems[d.eng], d.token_val
                if waited.get(sk, 0) >= val:
                    continue
                eng.wait_ge(sem, val)
                waited[sk] = val
            if o.fn is None:
                continue
            res = o.fn(eng)
            if o.is_dma:
                assert len(res) == o.ndma
                for ins in res:
                    ins.then_inc(dsems[o.dkey], 16)
            elif o.signal:
                res.then_inc(sems[engname], 1)


def small_layout(L):
    off = {}
    c = 0
    for name, n in (("cT", 24), ("normgT", L * 8), ("badaT", L * 24), ("convscT", L * 6),
                    ("convcfT", L * 62), ("cfbT", L * 2), ("lngT", L * 2), ("lnbT", L * 2),
                    ("ident", 128)):
        off[name] = c
        c += n
    return off, c


def build_program(ROWS, L, dbg=None):
    SEQ = ROWS * GW
    NBL = SEQ // BT
    assert NBL >= 3
    nc = bass.Bass("TRN2", target_bir_lowering=False)
    S = Sched()
    soff, NSM = small_layout(L)

    x_d = nc.dram_tensor("x", [2, SEQ, D], F32, kind="ExternalInput")
    ctx_d = nc.dram_tensor("ctx", [2, CTXL, D], F32, kind="ExternalInput")
    small_d = nc.dram_tensor("small", [P, NSM], F32, kind="ExternalInput")
    wada_d = nc.dram_tensor("w_ada", [L, D, 3 * D], F32, kind="ExternalInput")
    win_d = nc.dram_tensor("w_in", [L, D, DP], F32, kind="ExternalInput")
    wout_d = nc.dram_tensor("w_out", [L, D, D], F32, kind="ExternalInput")
    LL_d = nc.dram_tensor("LLb", [L, 8, 2, P, LLW], F32, kind="ExternalInput")
    fg_d = nc.dram_tensor("final_g", [1, D], F32, kind="ExternalInput")
    out_d = nc.dram_tensor("out", [2, SEQ, D], F32, kind="ExternalOutput")
    ctxs_d = nc.dram_tensor("ctxs", [2, CTXL, D], F32)
    dbg_d = {}
    if dbg:
        for name, shape in dbg.items():
            dbg_d[name] = nc.dram_tensor("dbg_" + name, list(shape), F32, kind="ExternalOutput")

    es = contextlib.ExitStack()
    with es:
        def sb(name, shape, dt):
            return es.enter_context(nc.sbuf_tensor(name, shape, dt))

        small = sb("small_sb", [P, NSM], F32)
        ident_bf = sb("ident_bf", [P, P], BF16)
        ones_bf = sb("ones_bf", [P, P], BF16)
        ones256 = sb("ones256", [P, P], F32)
        csT = sb("csT", [P, 24], F32)
        modT = sb("modT", [P, 24, 3], F32)
        Gm = sb("Gm", [P, KC, 3], F32)
        hln = sb("hln", [P, L * 4], F32)
        ssb = sb("ssb", [P, 2, 4], F32)
        rsb = sb("rsb", [P, 2, 4], F32)
        ssf = sb("ssf", [P, 2, 4], F32)
        grep_t = sb("grep_t", [P, 1, P], F32)
        win_sb = sb("win_sb", [P, KC, DP], BF16)
        wout_sb = sb("wout_sb", [P, KC, D], BF16)
        LLs = sb("LLs", [P, 2, LLW], BF16)
        dg_cf = sb("dg_cf", [P, 62, P], BF16)
        dg_sc = sb("dg_sc", [P, 6, P], BF16)
        scrA = sb("scrA", [P, 2, 2048], BF16)
        gate_bc = sb("gate_bc", [P, 3, D], F32)
        finalg_bc = sb("finalg_bc", [P, D], F32)
        xt = sb("xt", [P, 2, D], F32)
        xr = sb("xr", [P, 2, 512], F32)
        xn = sb("xn", [P, 2, D], BF16)
        hT = sb("hT", [P, 2, KC, BT], BF16)
        ub_sc = sb("ub_sc", [P, 3, 2, 288], BF16)
        ub_cf = sb("ub_cf", [P, 3, 2, 288], BF16)
        qT = sb("qTm", [P, 2, 2, 4, BT], BF16)
        kT = sb("kT", [P, 4, 4 * BT], BF16)
        Vr = sb("Vr", [P, 8, 512], BF16)
        kTc = sb("kTc", [P, 4, BT], BF16)
        Vc = sb("Vc", [P, 2, 512], BF16)
        ga = sb("ga", [P, 2, 4, BT], BF16)
        bg = sb("bg", [P, 2, 2, BT], BF16)
        cg = sb("cg", [P, 2, 2, BT], BF16)
        yT = sb("yT", [P, KC, BT], BF16)
        NTMP = 4
        tmp = sb("tmp", [P, NTMP, 512], F32)
        lnt = sb("lnt", [P, 2, 512], F32)
        ps = es.enter_context(nc.psum_tensor("ps", [P, 8, 512], F32))

        ident_f = small[:, soff["ident"]:soff["ident"] + P]

        sems = {e: es.enter_context(nc.semaphore("sem_" + e)) for e in ("pe", "act", "dve", "pool")}
        dsems = {}

        def dsem(key):
            if key not in dsems:
                dsems[key] = es.enter_context(nc.semaphore("ds_" + "_".join(str(k) for k in key)))
            return key

        cnt = {"tmp": 0, "gp": 0, "sps": 0, "pT": 0, "LL": 0, "xt": 0, "xr": 0, "wada": 0, "grep": 0}

        def nxt(name, n):
            i = cnt[name] % n
            cnt[name] += 1
            return i

        def new_tmp():
            i = nxt("tmp", NTMP)
            return tmp[:, i, :], ("tmp", i)

        def new_gp():
            i = nxt("gp", 4)
            return ps[:, i, :], ("gp", i)

        def new_sps():
            i = nxt("sps", 2)
            return ps[:, 4 + 2 * i:6 + 2 * i, :], ("sps", i)

        S.add("sp", lambda e: [e.dma_start(out=small[:, :], in_=small_d.ap()[:, :])],
              w=[("small",)], dkey=dsem(("small",)), ndma=1)
        S.add("sp", lambda e: [e.dma_start(out=finalg_bc[:, :], in_=fg_d.ap()[0:1, :].partition_broadcast(P))],
              w=[("finalg",)], dkey=dsem(("finalg",)), ndma=1)
        S.add("pool", lambda e: e.memset(ones_bf[:, :], 1.0), w=[("ones_bf",)])
        S.add("pool", lambda e: e.memset(ones256[:, :], 1.0 / 256.0), w=[("ones256",)])
        S.add("dve", lambda e: e.tensor_copy(out=ident_bf[:, :], in_=ident_f), r=[("small",)], w=[("ident_bf",)])
        cT_ap = small[:, soff["cT"]:soff["cT"] + 24]
        S.add("act", lambda e: e.activation(out=csT[:, :], in_=cT_ap, func=AF.Tanh, scale=0.5),
              r=[("small",)], w=[("csT",)])
        S.add("dve", lambda e: e.scalar_tensor_tensor(out=csT[:, :], in0=csT[:, :], scalar=1.0, in1=cT_ap,
                                                     op0=ALU.add, op1=ALU.mult), r=[("small",), ("csT",)], w=[("csT",)])
        S.add("dve", lambda e: e.tensor_scalar(out=csT[:, :], in0=csT[:, :], scalar1=0.5, scalar2=None, op0=ALU.mult),
              r=[("csT",)], w=[("csT",)])
        S.add("dve", lambda e: e.tensor_scalar(out=hln[:, 0:L * 2], in0=small[:, soff["lngT"]:soff["lngT"] + L * 2],
                                               scalar1=0.5, scalar2=None, op0=ALU.mult), r=[("small",)], w=[("hln",)])
        S.add("dve", lambda e: e.tensor_scalar(out=hln[:, L * 2:L * 4], in0=small[:, soff["lnbT"]:soff["lnbT"] + L * 2],
                                               scalar1=0.5, scalar2=None, op0=ALU.mult), r=[("small",), ("hln",)], w=[("hln",)])
        for sl_ in range(2):
            for e2_ in range(2):
                S.add("pool", (lambda e, sl_=sl_, e2_=e2_: e.memset(qT[:, sl_, e2_, :, :], 0.0)),
                      w=[("qT", sl_, 0, e2_), ("qT", sl_, 1, e2_)])
        for i in range(3):
            for (ub, nm) in ((ub_sc, "ubsc"), (ub_cf, "ubcf")):
                S.add("pool", (lambda e, ub=ub, i=i: e.memset(ub[:, i, :, :], 0.0)), w=[(nm, i)])

        def dump(name, ap, keys):
            if not dbg or name not in dbg_d:
                return
            d = dbg_d.pop(name)
            o = S.add("sp", lambda e: [e.dma_start(out=d.ap(), in_=ap)], r=keys, dkey=dsem(("dbg", name)), ndma=1)
            S.stores.append(o)

        def layer_prologue(l):
            def f_win(e):
                return [e.dma_start(out=win_sb[:, kc, :], in_=win_d.ap()[l, kc * P:(kc + 1) * P, :]) for kc in range(KC)]
            S.add("pool", f_win, w=[("win",)], dkey=dsem(("win",)), ndma=KC)

            def f_wout(e):
                return [e.dma_start(out=wout_sb[:, kc, :], in_=wout_d.ap()[l, kc * P:(kc + 1) * P, :]) for kc in range(KC)]
            S.add("pool", f_wout, w=[("wout",)], dkey=dsem(("wout",)), ndma=KC)
            for j in range(62):
                oc = soff["convcfT"] + l * 62 + j
                S.add("pool", (lambda e, j=j, oc=oc: e.tensor_scalar(out=dg_cf[:, j, :], in0=ident_bf[:, :],
                                                                     scalar1=small[:, oc:oc + 1], scalar2=0.5,
                                                                     op0=ALU.mult, op1=ALU.mult)),
                      r=[("ident_bf",), ("small",)], w=[("dgcf", j)])
            for j in range(6):
                o0 = soff["convscT"] + l * 6 + j
                S.add("pool", (lambda e, j=j, o0=o0: e.tensor_scalar(out=dg_sc[:, j, :], in0=ident_bf[:, :],
                                                                     scalar1=small[:, o0:o0 + 1], scalar2=None,
                                                                     op0=ALU.mult)),
                      r=[("ident_bf",), ("small",)], w=[("dgsc", j)])
            for ch in range(24):
                wi = nxt("wada", 2)
                wsl = scrA[:, wi, :].bitcast(F32).rearrange("p (k n) -> p k n", k=KC)
                src = wada_d.ap()[l, :, ch * P:(ch + 1) * P].rearrange("(k p) n -> p k n", p=P)
                S.add("sp", (lambda e, wsl=wsl, src=src: [e.dma_start(out=wsl, in_=src)]),
                      w=[("pT", wi, 0), ("pT", wi, 1)], dkey=dsem(("wada", wi)), ndma=1)
                gpa, gpk = new_gp()

                def f_mod(e, wsl=wsl, gpa=gpa):
                    ins = None
                    for kc in range(KC):
                        ins = e.matmul(gpa[:, 0:3], lhsT=wsl[:, kc, :], rhs=csT[:, kc * 3:(kc + 1) * 3],
                                       start=(kc == 0), stop=(kc == KC - 1))
                    return ins
                S.add("pe", f_mod, r=[("pT", wi, 0), ("pT", wi, 1), ("csT",)], w=[gpk])
                ob = soff["badaT"] + l * 24 + ch
                S.add("dve", (lambda e, gpa=gpa, ch=ch, ob=ob: e.tensor_scalar(
                    out=modT[:, ch, :], in0=gpa[:, 0:3], scalar1=small[:, ob:ob + 1], scalar2=None, op0=ALU.add)),
                    r=[gpk, ("small",)], w=[("modT", ch)])
            for kc in range(KC):
                og = soff["normgT"] + l * 8 + kc
                S.add("dve", (lambda e, kc=kc, og=og: e.tensor_scalar(
                    out=Gm[:, kc, :], in0=modT[:, 8 + kc, :], scalar1=1.0, scalar2=small[:, og:og + 1],
                    op0=ALU.add, op1=ALU.mult)), r=[("modT", 8 + kc), ("small",)], w=[("Gm", kc)])
            for ch in range(8):
                for m in range(3):
                    gi = nxt("grep", 1)
                    S.add("dve", (lambda e, gi=gi, ch=ch, m=m: e.tensor_scalar(
                        out=grep_t[:, gi, :], in0=ones256[:, :], scalar1=modT[:, 16 + ch, m:m + 1], scalar2=128.0,
                        op0=ALU.mult, op1=ALU.mult)), r=[("modT", 16 + ch), ("ones256",)], w=[("grep", gi)])
                    gpa, gpk = new_gp()
                    S.add("pe", (lambda e, gi=gi, gpa=gpa: e.matmul(gpa[:, 0:P], lhsT=grep_t[:, gi, :], rhs=ident_f,
                                                                    start=True, stop=True)),
                          r=[("grep", gi), ("small",)], w=[gpk])
                    S.add("act", (lambda e, gpa=gpa, ch=ch, m=m: e.activation(
                        out=gate_bc[:, m, ch * P:(ch + 1) * P], in_=gpa[:, 0:P], func=AF.Copy)),
                        r=[gpk], w=[("gate_bc", m)])

        xt_slots = {}

        def stage1a(l, b, kind, n, nblocks, fi, part="both"):
            is_ctx = kind == "c"
            m = 2 if is_ctx else b
            if l == 0:
                src_d = ctx_d if is_ctx else x_d
            else:
                src_d = ctxs_d if is_ctx else out_d
            dkey_x = ("xd", kind, b, n)
            slot = fi % 2
            hs = fi % 2
            if part in ("load", "both"):
                xts = []
                for t in range(2):
                    xi = nxt("xt", 2)
                    xts.append(xi)
                    r0 = n * BT + t * P
                    S.add("sp", (lambda e, xi=xi, r0=r0: [e.dma_start(out=xt[:, xi, :], in_=src_d.ap()[b, r0:r0 + P, :])]),
                          r=[dkey_x], w=[("xt", xi)], dkey=dsem(("xt", xi)), ndma=1)
                xt_slots[(l, fi)] = xts
                if part == "load":
                    return
            xts = xt_slots.get((l, fi), [])
            for t in range(2):
                if part == "back":
                    break
                xi = xts[t]
                S.add("act", (lambda e, xi=xi, t=t: e.activation(out=xn[:, t, :], in_=xt[:, xi, :], func=AF.Square,
                                                                 accum_out=ssb[:, slot, t:t + 1])),
                      r=[("xt", xi)], w=[("xn", t), ("ssb", slot, t)])
            if part != "back":
              S.add("dve", lambda e: e.tensor_scalar(out=rsb[:, slot, 0:2], in0=ssb[:, slot, 0:2], scalar1=1.0 / D,
                                                   scalar2=NORM_EPS, op0=ALU.mult, op1=ALU.add),
                  r=[("ssb", slot, 0), ("ssb", slot, 1)], w=[("rsb", slot)])
              S.add("dve", lambda e: e.reciprocal(out=rsb[:, slot, 0:2], in_=rsb[:, slot, 0:2]), r=[("rsb", slot)], w=[("rsb", slot)])
              S.add("act", lambda e: e.activation(out=rsb[:, slot, 0:2], in_=rsb[:, slot, 0:2], func=AF.Sqrt),
                    r=[("rsb", slot)], w=[("rsb", slot)])
              for t in range(2):
                xi = xts[t]
                S.add("dve", (lambda e, xi=xi, t=t: e.tensor_scalar(out=xn[:, t, :], in0=xt[:, xi, :],
                                                                    scalar1=rsb[:, slot, t:t + 1], scalar2=None, op0=ALU.mult)),
                      r=[("xt", xi), ("rsb", slot)], w=[("xn", t)])
            for t in range(2):
                if part == "front":
                    break
                gpa, gpk = new_gp()
                tp = gpa.bitcast(BF16)

                def f_tr(e, t=t, tp=tp):
                    ins = None
                    for kc in range(KC):
                        ins = e.transpose(out=tp[:, kc * P:(kc + 1) * P], in_=xn[:, t, kc * P:(kc + 1) * P],
                                          identity=ident_bf[:, :])
                    return ins
                S.add("pe", f_tr, r=[("xn", t), ("ident_bf",)], w=[gpk])
                hcf, hck = new_tmp()
                hc = hcf.bitcast(BF16)
                S.add("dve", (lambda e, tp=tp, hc=hc: e.tensor_copy(out=hc, in_=tp)), r=[gpk], w=[hck])
                for kc in range(KC):
                    if kc % 2 == 0:
                        S.add("pool", (lambda e, kc=kc, t=t, hc=hc: e.tensor_scalar(
                            out=hT[:, hs, kc, t * P:(t + 1) * P], in0=hc[:, kc * P:(kc + 1) * P],
                            scalar1=Gm[:, kc, m:m + 1], scalar2=modT[:, kc, m:m + 1], op0=ALU.mult, op1=ALU.add)),
                            r=[hck, ("Gm", kc), ("modT", kc)], w=[("hT", hs, t, kc)])
                    else:
                        S.add("dve", (lambda e, kc=kc, t=t, hc=hc: e.tensor_scalar(
                            out=hT[:, hs, kc, t * P:(t + 1) * P], in0=hc[:, kc * P:(kc + 1) * P],
                            scalar1=Gm[:, kc, m:m + 1], scalar2=modT[:, kc, m:m + 1], op0=ALU.mult, op1=ALU.add)),
                            r=[hck, ("Gm", kc), ("modT", kc)], w=[("hT", hs, t, kc)])

        def stage1b(l, b, kind, n, nblocks, fi):
            is_ctx = kind == "c"
            slot = fi % 2
            hs = fi % 2
            us = fi % 3
            hkeys = [("hT", hs, t, kc) for t in range(2) for kc in range(KC)]
            if dbg and l == dbg.get("_l", 0) and b == 0 and n == dbg.get("_n", 0) and (is_ctx == dbg.get("_ctx", False)):
                pass

            def proj_pair(fc0):
                gpa, gpk = new_gp()

                def f(e, gpa=gpa):
                    ins = None
                    for c in range(2):
                        fc = fc0 + c
                        for kc in range(KC):
                            ins = e.matmul(gpa[:, c * BT:(c + 1) * BT], lhsT=win_sb[:, kc, fc * P:(fc + 1) * P],
                                           rhs=hT[:, hs, kc, :], start=(kc == 0), stop=(kc == KC - 1))
                    return ins
                S.add("pe", f, r=hkeys + [("win",)], w=[gpk])
                return gpa, gpk

            def v3(ap2):
                return ap2.rearrange("p (c t) -> p c t", c=2)

            gpa, gpk = proj_pair(0)
            zh, zhk = new_tmp()
            S.add("act", (lambda e, gpa=gpa, zh=zh: e.activation(out=zh, in_=gpa, func=AF.Copy)), r=[gpk], w=[zhk])
            gpa, gpk = proj_pair(4)
            S.add("dve", (lambda e, gpa=gpa, zh=zh: e.tensor_tensor(out=ub_sc[:, us, :, 16:272], in0=v3(gpa), in1=v3(zh),
                                                                   op=ALU.mult)), r=[gpk, zhk], w=[("ubsc", us)])
            gpa, gpk = proj_pair(6)
            th, thk = new_tmp()
            S.add("act", (lambda e, gpa=gpa, th=th: e.activation(out=th, in_=gpa, func=AF.Tanh, scale=0.5)), r=[gpk], w=[thk])
            S.add("dve", (lambda e, gpa=gpa, th=th: e.scalar_tensor_tensor(out=th, in0=th, scalar=1.0, in1=gpa,
                                                                          op0=ALU.add, op1=ALU.mult)), r=[gpk, thk], w=[thk])
            gpa, gpk = proj_pair(2)
            S.add("dve", (lambda e, gpa=gpa, th=th: e.tensor_tensor(out=bg[:, slot, :, :], in0=v3(gpa), in1=v3(th),
                                                                   op=ALU.mult)), r=[gpk, thk], w=[("bg", slot)])
            gpa, gpk = proj_pair(26)
            th, thk = new_tmp()
            S.add("act", (lambda e, gpa=gpa, th=th: e.activation(out=th, in_=gpa, func=AF.Tanh, scale=0.5)), r=[gpk], w=[thk])
            gpa, gpk = proj_pair(24)
            S.add("dve", (lambda e, gpa=gpa, th=th: e.scalar_tensor_tensor(
                out=ub_cf[:, us, :, 16:272], in0=v3(th), scalar=1.0, in1=v3(gpa), op0=ALU.add, op1=ALU.mult)),
                r=[gpk, thk], w=[("ubcf", us)])
            gpa, gpk = proj_pair(28)
            th, thk = new_tmp()
            S.add("act", (lambda e, gpa=gpa, th=th: e.activation(out=th, in_=gpa, func=AF.Tanh, scale=0.5)), r=[gpk], w=[thk])
            S.add("dve", (lambda e, gpa=gpa, th=th: e.scalar_tensor_tensor(
                out=cg[:, slot, :, :], in0=v3(th), scalar=1.0, in1=v3(gpa), op0=ALU.add, op1=ALU.mult)),
                r=[gpk, thk], w=[("cg", slot)])
            for (ub, nm) in ((ub_sc, "ubsc"), (ub_cf, "ubcf")):
                if n == 0:
                    S.add("pool", (lambda e, ub=ub: e.memset(ub[:, us, :, 0:16], 0.0)), w=[(nm, us)])
                else:
                    pu = (fi - 1) % 3
                    S.add("pool", (lambda e, ub=ub, pu=pu: e.tensor_copy(out=ub[:, pu, :, 272:288], in_=ub[:, us, :, 16:32])),
                          r=[(nm, us)], w=[(nm, pu)])
                if n == nblocks - 1:
                    S.add("pool", (lambda e, ub=ub: e.memset(ub[:, us, :, 272:288], 0.0)), w=[(nm, us)])
                else:
                    nu = (fi + 1) % 3
                    S.add("pool", (lambda e, ub=ub, nu=nu: e.tensor_copy(out=ub[:, nu, :, 0:16], in_=ub[:, us, :, 256:272])),
                          r=[(nm, us)], w=[(nm, nu)])
            for i in range(2):
                gpa, gpk = proj_pair(8 + 2 * i)
                S.add("act", (lambda e, gpa=gpa, i=i: e.activation(out=qT[0:64, slot, 0, 2 * i:2 * i + 2, :], in_=v3(gpa)[0:64],
                                                                   func=AF.Identity, scale=0.125)), r=[gpk], w=[("qT", slot, i, 0)])
                S.add("dve", (lambda e, gpa=gpa, i=i: e.tensor_scalar(out=qT[64:128, slot, 1, 2 * i:2 * i + 2, :], in0=v3(gpa)[64:128],
                                                                      scalar1=0.125, scalar2=None, op0=ALU.mult)),
                      r=[gpk], w=[("qT", slot, i, 1)])
            for i in range(2):
                gpa, gpk = proj_pair(12 + 2 * i)
                if is_ctx:
                    dst, dk = kTc[:, 2 * i:2 * i + 2, :], ("kTc", i)
                else:
                    dst, dk = kT[:, 2 * i:2 * i + 2, (n % 4) * BT:(n % 4 + 1) * BT], ("kT", n % 4, i)
                S.add("dve", (lambda e, gpa=gpa, dst=dst: e.tensor_copy(out=dst, in_=v3(gpa))), r=[gpk], w=[dk])
            for t in range(2):
                gpa, gpk = new_gp()

                def f(e, gpa=gpa, t=t):
                    ins = None
                    for kc in range(KC):
                        ins = e.matmul(gpa, lhsT=hT[:, hs, kc, t * P:(t + 1) * P], rhs=win_sb[:, kc, 2048:2560],
                                       start=(kc == 0), stop=(kc == KC - 1))
                    return ins
                S.add("pe", f, r=hkeys + [("win",)], w=[gpk])
                if is_ctx:
                    dst, dk = Vc[:, t, :], ("Vc", t)
                else:
                    dst, dk = Vr[:, 2 * (n % 4) + t, :], ("V", n % 4, t)
                if t == 0:
                    S.add("act", (lambda e, gpa=gpa, dst=dst: e.activation(out=dst, in_=gpa, func=AF.Copy)), r=[gpk], w=[dk])
                else:
                    S.add("dve", (lambda e, gpa=gpa, dst=dst: e.tensor_copy(out=dst, in_=gpa)), r=[gpk], w=[dk])
            for i in range(2):
                gpa, gpk = proj_pair(20 + 2 * i)
                th, thk = new_tmp()
                S.add("act", (lambda e, gpa=gpa, th=th: e.activation(out=th, in_=gpa, func=AF.Tanh, scale=0.5)), r=[gpk], w=[thk])
                S.add("dve", (lambda e, gpa=gpa, th=th, i=i: e.scalar_tensor_tensor(
                    out=ga[:, slot, 2 * i:2 * i + 2, :], in0=v3(th), scalar=1.0, in1=v3(gpa), op0=ALU.add, op1=ALU.mult)),
                    r=[gpk, thk], w=[("ga", slot, i)])

        def stage2(l, b, kind, n, nblocks, fi, hook=None, hook_front=None):
            is_ctx = kind == "c"
            last_layer = l == L - 1
            m = 2 if is_ctx else b
            if l == 0:
                src_d = ctx_d if is_ctx else x_d
            else:
                src_d = ctxs_d if is_ctx else out_d
            dst_d = ctxs_d if is_ctx else out_d
            dkey_x = ("xd", kind, b, n)
            slot = fi % 2
            us = fi % 3
            if is_ctx:
                wch = []
                ty = None
            elif n == 0:
                wch = [(c, (6 - 2 * c) * 64) for c in range(4)]
                ty = 0
            elif n == nblocks - 1:
                wch = [(2 * nblocks - 4 + c, (10 - 2 * c) * 64) for c in range(4)]
                ty = 0
            else:
                wch = [(2 * n - 2 + d, (10 - 2 * d) * 64) for d in range(6)]
                ty = 1
            chunks = [("w", ch, off) for (ch, off) in wch] + [("c", 0, None), ("c", 1, None)]
            groups = [chunks[i:i + 4] for i in range(0, len(chunks), 4)]
            ll_slot = {}
            xr_slots = {}

            def load_xr(t_):
                for hf_ in range(2):
                    xi = nxt("xr", 2)
                    xr_slots[(t_, hf_)] = xi
                    rr0 = n * BT + t_ * P
                    S.add("sp", (lambda e, xi=xi, rr0=rr0, hf_=hf_: [e.dma_start(
                        out=xr[:, xi, :], in_=src_d.ap()[b, rr0:rr0 + P, hf_ * 512:(hf_ + 1) * 512])]),
                        r=[dkey_x], w=[("xr", xi)], dkey=dsem(("xr", xi)), ndma=1)
            load_xr(0)

            def issue_LL(h):
                if is_ctx or h >= 8:
                    return
                li = h % 2
                ll_slot[h] = li
                S.add("pool", (lambda e, li=li, h=h: [e.dma_start(out=LLs[:, li, :], in_=LL_d.ap()[l, h, ty, :, :])]),
                      w=[("LL", li)], dkey=dsem(("LL", li)), ndma=1)

            hstate = {}

            def attn_qk(h):
                hp, e_ = h // 2, h % 2
                r0, r1 = 64 * e_, 64 * e_ + 64
                li = ll_slot.get(h, 0)
                pi = nxt("pT", 2)
                pT = scrA[:, pi, :]
                ci = 0
                ginfo = []
                for gi, g in enumerate(groups):
                    spa, spk = new_sps()
                    spf = spa.rearrange("p a b -> p (a b)")
                    rk = [("qT", slot, hp // 2, e_)]
                    for (kind_c, ch, off) in g:
                        if kind_c == "w":
                            rk += [("kT", (ch // 2) % 4, hp // 2), ("LL", li), ("ident_bf",)]
                        else:
                            rk += [("kTc", hp // 2)]

                    def f_qk(e, g=g, spf=spf, hp=hp, r0=r0, r1=r1, li=li, e_=e_):
                        ins = None
                        for j, (kind_c, ch, off) in enumerate(g):
                            o_ = spf[:, j * BT:(j + 1) * BT]
                            if kind_c == "w":
                                rc = (ch % 8) * P
                                e.matmul(o_, lhsT=kT[:, hp, rc:rc + P], rhs=qT[:, slot, e_, hp, :],
                                         start=True, stop=False)
                                ins = e.matmul(o_, lhsT=ident_bf[:, :], rhs=LLs[:, li, off:off + BT],
                                               start=False, stop=True)
                            else:
                                ins = e.matmul(o_, lhsT=kTc[:, hp, ch * P:(ch + 1) * P], rhs=qT[:, slot, e_, hp, :],
                                               start=True, stop=True)
                        return ins
                    S.add("pe", f_qk, r=rk, w=[spk])
                    ng = len(g)
                    S.add("act", (lambda e, spf=spf, pT=pT, ci=ci, ng=ng: e.activation(
                        out=pT[:, ci * BT:(ci + ng) * BT], in_=spf[:, 0:ng * BT], func=AF.Exp)),
                        r=[spk], w=[("pT", pi, gi)])
                    gl = []
                    for (kind_c, ch, off) in g:
                        gl.append((kind_c, ch, ci))
                        ci += 1
                    ginfo.append(gl)
                issue_LL(h + 2)
                hstate[h] = (pi, pT, ginfo, ci)

            def attn_pv(h):
                hp, e_ = h // 2, h % 2
                r0, r1 = 64 * e_, 64 * e_ + 64
                pi, pT, ginfo, ci = hstate.pop(h)
                gpa, gpk = new_gp()
                nch = ci
                for gi, gl in enumerate(ginfo):
                    rk = [("pT", pi, gi), ("ones_bf",)]
                    for (kind_c, ch, cpos) in gl:
                        rk.append(("V", (ch // 2) % 4, ch % 2) if kind_c == "w" else ("Vc", ch))

                    def f_pv(e, gl=gl, gpa=gpa, pT=pT, hp=hp, nch=nch, last=(gi == len(ginfo) - 1)):
                        ins = None
                        for (kind_c, ch, cpos) in gl:
                            lw = Vr[:, ch % 8, hp * P:(hp + 1) * P] if kind_c == "w" else Vc[:, ch, hp * P:(hp + 1) * P]
                            ins = e.matmul(gpa[:, 0:BT], lhsT=lw, rhs=pT[:, cpos * BT:(cpos + 1) * BT],
                                           start=(cpos == 0), stop=(cpos == nch - 1))
                        if last:
                            for cpos in range(nch):
                                ins = e.matmul(gpa[:, BT:2 * BT], lhsT=ones_bf[:, :], rhs=pT[:, cpos * BT:(cpos + 1) * BT],
                                               start=(cpos == 0), stop=(cpos == nch - 1))
                        return ins
                    S.add("pe", f_pv, r=rk + ([gpk] if gi > 0 else []) + ([("pT", pi, j) for j in range(len(ginfo))] if gi == len(ginfo) - 1 else []),
                          w=[gpk])
                t1, t1k = new_tmp()
                S.add("dve", (lambda e, gpa=gpa, t1=t1, r0=r0, r1=r1: e.reciprocal(out=t1[r0:r1, 0:BT], in_=gpa[r0:r1, BT:2 * BT])),
                      r=[gpk], w=[t1k])
                S.add("dve", (lambda e, gpa=gpa, t1=t1, r0=r0, r1=r1: e.tensor_tensor(
                    out=t1[r0:r1, BT:2 * BT], in0=gpa[r0:r1, 0:BT], in1=t1[r0:r1, 0:BT], op=ALU.mult)),
                    r=[gpk, t1k], w=[t1k])
                S.add("pool", (lambda e, t1=t1, r0=r0, r1=r1, hp=hp: e.tensor_tensor(
                    out=yT[r0:r1, 2 + hp, :], in0=t1[r0:r1, BT:2 * BT], in1=ga[r0:r1, slot, hp, :], op=ALU.mult)),
                    r=[t1k, ("ga", slot, hp // 2)], w=[("yT", 2 + hp, e_)])

            issue_LL(0)
            issue_LL(1)
            gpc, gpck = new_gp()

            def f_cf(e, gpa=gpc):
                ins = None
                for c in range(2):
                    for k in range(31):
                        ins = e.matmul(gpa[:, c * BT:(c + 1) * BT], lhsT=dg_cf[:, c * 31 + k, :],
                                       rhs=ub_cf[:, us, c, 1 + k:1 + k + BT], start=(k == 0), stop=(k == 30))
                return ins
            S.add("pe", f_cf, r=[("ubcf", us)] + [("dgcf", j) for j in range(62)], w=[gpck])
            vcf, vcfk = lnt[:, 0, :], ("lnt", 0)
            vsq, vsqk = lnt[:, 1, :], ("lnt", 1)
            for c in range(2):
                ob = soff["cfbT"] + l * 2 + c
                S.add("act", (lambda e, gpa=gpc, vcf=vcf, c=c, ob=ob: e.activation(
                    out=vcf[:, c * BT:(c + 1) * BT], in_=gpa[:, c * BT:(c + 1) * BT], func=AF.Identity,
                    bias=small[:, ob:ob + 1])), r=[gpck, ("small",)], w=[vcfk])
                S.add("act", (lambda e, gpa=gpc, vsq=vsq, c=c, ob=ob: e.activation(
                    out=vsq[:, c * BT:(c + 1) * BT], in_=gpa[:, c * BT:(c + 1) * BT], func=AF.Square,
                    bias=small[:, ob:ob + 1])), r=[gpck, ("small",)], w=[vsqk])
            gpa, gpk = new_gp()

            def f_sc(e, gpa=gpa):
                ins = None
                for c in range(2):
                    for k in range(3):
                        ins = e.matmul(gpa[:, c * BT:(c + 1) * BT], lhsT=dg_sc[:, c * 3 + k, :],
                                       rhs=ub_sc[:, us, c, 15 + k:15 + k + BT], start=(k == 0), stop=(k == 2))
                return ins
            S.add("pe", f_sc, r=[("ubsc", us)] + [("dgsc", j) for j in range(6)], w=[gpk])
            S.add("dve", (lambda e, gpa=gpa: e.tensor_tensor(out=yT[:, 0:2, :], in0=gpa.rearrange("p (c t) -> p c t", c=2),
                                                            in1=bg[:, slot, :, :], op=ALU.mult)),
                  r=[gpk, ("bg", slot)], w=[("yT", 0, 0), ("yT", 1, 0)])
            attn_qk(0)
            attn_qk(1)
            attn_pv(0)
            attn_qk(2)
            attn_pv(1)
            gps, gpsk = new_gp()

            def f_st(e, gps=gps, vcf=vcf, vsq=vsq):
                e.matmul(gps[:, 0:BT], lhsT=ones256[:, :], rhs=vcf[:, 0:BT], start=True, stop=False)
                e.matmul(gps[:, 0:BT], lhsT=ones256[:, :], rhs=vcf[:, BT:2 * BT], start=False, stop=True)
                e.matmul(gps[:, BT:2 * BT], lhsT=ones256[:, :], rhs=vsq[:, 0:BT], start=True, stop=False)
                return e.matmul(gps[:, BT:2 * BT], lhsT=ones256[:, :], rhs=vsq[:, BT:2 * BT], start=False, stop=True)
            S.add("pe", f_st, r=[vcfk, vsqk, ("ones256",)], w=[gpsk])
            st, stk = lnt[:, 1, :], ("lnt", 1)
            S.add("act", (lambda e, gps=gps, st=st: e.activation(out=st[:, 0:BT], in_=gps[:, 0:BT], func=AF.Square)),
                  r=[gpsk], w=[stk])
            S.add("dve", (lambda e, gps=gps, st=st: e.tensor_tensor(out=st[:, 0:BT], in0=gps[:, BT:2 * BT], in1=st[:, 0:BT],
                                                                   op=ALU.subtract)), r=[gpsk, stk], w=[stk])
            S.add("dve", (lambda e, st=st: e.tensor_scalar(out=st[:, 0:BT], in0=st[:, 0:BT], scalar1=LN_EPS, scalar2=None,
                                                          op0=ALU.add)), r=[stk], w=[stk])
            S.add("dve", (lambda e, st=st: e.reciprocal(out=st[:, 0:BT], in_=st[:, 0:BT])), r=[stk], w=[stk])
            S.add("act", (lambda e, gps=gps, st=st: e.activation(out=st[:, BT:2 * BT], in_=gps[:, 0:BT], func=AF.Copy)),
                  r=[gpsk, stk], w=[stk])
            attn_qk(3)
            attn_pv(2)
            attn_qk(4)
            attn_pv(3)
            if hook_front is not None:
                hook_front()
            S.add("act", (lambda e, st=st: e.activation(out=st[:, 0:BT], in_=st[:, 0:BT], func=AF.Sqrt)), r=[stk], w=[stk])
            for c in range(2):
                S.add("dve", (lambda e, vcf=vcf, st=st, c=c: e.tensor_tensor(
                    out=vcf[:, c * BT:(c + 1) * BT], in0=vcf[:, c * BT:(c + 1) * BT], in1=st[:, BT:2 * BT], op=ALU.subtract)),
                    r=[vcfk, stk], w=[vcfk])
                S.add("pool", (lambda e, vcf=vcf, st=st, c=c: e.tensor_tensor(
                    out=vcf[:, c * BT:(c + 1) * BT], in0=vcf[:, c * BT:(c + 1) * BT], in1=st[:, 0:BT], op=ALU.mult)),
                    r=[vcfk, stk], w=[vcfk])
                S.add("dve", (lambda e, vcf=vcf, c=c: e.tensor_scalar(
                    out=vcf[:, c * BT:(c + 1) * BT], in0=vcf[:, c * BT:(c + 1) * BT],
                    scalar1=hln[:, l * 2 + c:l * 2 + c + 1], scalar2=hln[:, L * 2 + l * 2 + c:L * 2 + l * 2 + c + 1],
                    op0=ALU.mult, op1=ALU.add)), r=[vcfk, ("hln",)], w=[vcfk])
            attn_qk(5)
            attn_pv(4)
            attn_qk(6)
            attn_pv(5)
            S.add("act", (lambda e, vcf=vcf, vsq=vsq: e.activation(out=vsq, in_=vcf, func=AF.Tanh)), r=[vcfk, vsqk], w=[vsqk])
            S.add("dve", (lambda e, vcf=vcf, vsq=vsq: e.scalar_tensor_tensor(out=vsq, in0=vsq, scalar=1.0, in1=vcf,
                                                                            op0=ALU.add, op1=ALU.mult)),
                  r=[vcfk, vsqk], w=[vsqk])
            S.add("pool", (lambda e, vsq=vsq: e.tensor_tensor(out=yT[:, 6:8, :], in0=vsq.rearrange("p (c t) -> p c t", c=2),
                                                             in1=cg[:, slot, :, :], op=ALU.mult)),
                  r=[vsqk, ("cg", slot)], w=[("yT", 6, 0), ("yT", 7, 0)])
            if hook is not None:
                hook()
            attn_qk(7)
            attn_pv(6)
            attn_pv(7)
            ykeys = [("yT", 0, 0), ("yT", 1, 0), ("yT", 6, 0), ("yT", 7, 0)] + \
                    [("yT", 2 + hp, e_) for hp in range(4) for e_ in range(2)]
            if dbg and "yT" in dbg_d and l == dbg.get("_l", 0) and b == 0 and n == dbg.get("_n", 0) and (is_ctx == dbg.get("_ctx", False)):
                pass
            fin_norm = last_layer and not is_ctx
            pend_all = {}
            for t in range(2):
                pend = []
                for hf in range(2):
                    gpa, gpk = new_gp()

                    def f_oa(e, gpa=gpa, t=t, hf=hf):
                        ins = None
                        for j, kc in enumerate((0, 1, 6, 7, 2, 3, 4)):
                            ins = e.matmul(gpa, lhsT=yT[:, kc, t * P:(t + 1) * P], rhs=wout_sb[:, kc, hf * 512:(hf + 1) * 512],
                                           start=(j == 0), stop=False)
                        return ins

                    def f_ob(e, gpa=gpa, t=t, hf=hf):
                        return e.matmul(gpa, lhsT=yT[:, 5, t * P:(t + 1) * P], rhs=wout_sb[:, 5, hf * 512:(hf + 1) * 512],
                                        start=False, stop=True)
                    ykA = [k for k in ykeys if k[1] != 5]
                    ykB = [k for k in ykeys if k[1] == 5]
                    S.add("pe", f_oa, r=ykA + [("wout",)], w=[gpk])
                    pend.append((f_ob, ykB, gpa, gpk, hf))
                pend_all[t] = pend
            for t in range(2):
                r0 = n * BT + t * P
                xnews = []
                pend = pend_all[t]
                for (f_ob, ykB, gpa, gpk, hf) in pend:
                    S.add("pe", f_ob, r=ykB + [("wout",), gpk], w=[gpk])
                    xi = xr_slots[(t, hf)]
                    xw, xwk = new_tmp()
                    S.add("dve", (lambda e, gpa=gpa, xw=xw, hf=hf: e.tensor_tensor(
                        out=xw, in0=gpa, in1=gate_bc[:, m, hf * 512:(hf + 1) * 512], op=ALU.mult)),
                        r=[gpk, ("gate_bc", m)], w=[xwk])
                    S.add("pool", (lambda e, xw=xw, xi=xi: e.tensor_tensor(out=xw, in0=xw, in1=xr[:, xi, :], op=ALU.add)),
                          r=[xwk, ("xr", xi)], w=[xwk])
                    xnews.append((xw, xwk, hf))
                if t == 0:
                    load_xr(1)
                if fin_norm:
                    fs = (n * 2 + t) % 2
                    jk, jkk = new_tmp()
                    for (xw, xwk, hf) in xnews:
                        S.add("act", (lambda e, xw=xw, jk=jk, hf=hf, fs=fs: e.activation(
                            out=jk, in_=xw, func=AF.Square, accum_out=ssf[:, fs, hf:hf + 1])),
                            r=[xwk], w=[jkk, ("ssf", fs, hf)])
                    S.add("dve", (lambda e, fs=fs: e.tensor_tensor(out=ssf[:, fs, 2:3], in0=ssf[:, fs, 0:1], in1=ssf[:, fs, 1:2],
                                                                  op=ALU.add)), r=[("ssf", fs, 0), ("ssf", fs, 1)], w=[("ssf", fs, 2)])
                    S.add("dve", (lambda e, fs=fs: e.tensor_scalar(out=ssf[:, fs, 2:3], in0=ssf[:, fs, 2:3], scalar1=1.0 / D,
                                                                  scalar2=NORM_EPS, op0=ALU.mult, op1=ALU.add)),
                          r=[("ssf", fs, 2)], w=[("ssf", fs, 2)])
                    S.add("dve", (lambda e, fs=fs: e.reciprocal(out=ssf[:, fs, 2:3], in_=ssf[:, fs, 2:3])),
                          r=[("ssf", fs, 2)], w=[("ssf", fs, 2)])
                    S.add("act", (lambda e, fs=fs: e.activation(out=ssf[:, fs, 3:4], in_=ssf[:, fs, 2:3], func=AF.Sqrt)),
                          r=[("ssf", fs, 2)], w=[("ssf", fs, 3)])
                    for (xw, xwk, hf) in xnews:
                        S.add("dve", (lambda e, xw=xw, hf=hf, fs=fs: e.scalar_tensor_tensor(
                            out=xw, in0=xw, scalar=ssf[:, fs, 3:4], in1=finalg_bc[:, hf * 512:(hf + 1) * 512],
                            op0=ALU.mult, op1=ALU.mult)), r=[xwk, ("ssf", fs, 3), ("finalg",)], w=[xwk])
                for (xw, xwk, hf) in xnews:
                    o = S.add("sp", (lambda e, xw=xw, r0=r0, hf=hf: [e.dma_start(
                        out=dst_d.ap()[b, r0:r0 + P, hf * 512:(hf + 1) * 512], in_=xw)]),
                        r=[xwk], w=[dkey_x], dkey=dsem(("st", xwk[1])), ndma=1)
                    S.stores.append(o)

        for l in range(L):
            layer_prologue(l)
            blks = []
            for b in range(2):
                blks.append((b, "c", 0, 1))
                for n in range(NBL):
                    blks.append((b, "x", n, NBL))
            M = len(blks)
            stage1a(l, *blks[0], 0)
            stage1a(l, *blks[1], 1)
            stage1b(l, *blks[0], 0)
            stage1a(l, *blks[2], 2, part="load")
            for i in range(M):
                hook = None
                hook_front = None
                if i + 2 < M:
                    def hook_front(i=i):
                        stage1a(l, *blks[i + 2], i + 2, part="front")
                        if i + 3 < M:
                            stage1a(l, *blks[i + 3], i + 3, part="load")
                    hook = (lambda i=i: stage1a(l, *blks[i + 2], i + 2, part="back"))
                late = (i + 1 < M and blks[i + 1][1] == "c")
                if i + 1 < M and not late:
                    stage1b(l, *blks[i + 1], i + 1)
                if not (blks[i][1] == "c" and l == L - 1):
                    stage2(l, *blks[i], i, hook=hook, hook_front=hook_front)
                else:
                    if hook_front is not None:
                        hook_front()
                    if hook is not None:
                        hook()
                if late:
                    stage1b(l, *blks[i + 1], i + 1)
        fin = S.add("sp", None)
        fin.deps = list(S.stores)
        S.finalize()

        block = es.enter_context(nc.Block())

        @block.tensor
        def _(e):
            S.emit_engine("pe", e, sems, dsems)

        @block.scalar
        def _(e):
            S.emit_engine("act", e, sems, dsems)

        @block.vector
        def _(e):
            S.emit_engine("dve", e, sems, dsems)

        @block.gpsimd
        def _(e):
            S.emit_engine("pool", e, sems, dsems)

        @block.sync
        def _(e):
            S.emit_engine("sp", e, sems, dsems)

    return nc, S


def build_LL(rpb):
    L = rpb.shape[0]
    kc = np.arange(64)[:, None]
    qc = np.arange(64)[None, :]
    cs = np.clip(qc - 8, 0, 48)
    colvalid = (kc >= cs) & (kc < cs + 16)
    cidx = np.clip(kc - qc + 15, 0, 30)
    LL = np.full((L, 8, 2, 2, 64, 14, 64), NEG, np.float32)
    for krl in range(2):
        for si, s in enumerate(range(-6, 8)):
            dr = krl - s
            if abs(dr) > 7:
                continue
            blk = rpb[:, :, dr + 7, :][:, :, cidx]
            blk = np.where(colvalid, blk, np.float32(NEG)).astype(np.float32)
            LL[:, :, 0, krl, :, si, :] = blk
            if -4 <= dr <= 3:
                LL[:, :, 1, krl, :, si, :] = blk
    return np.ascontiguousarray(LL.reshape(L, 8, 2, P, LLW))


def fm(v, nchunk):
    v = np.asarray(v, np.float32)
    lead = v.shape[:-1]
    v = v.reshape(lead + (nchunk, P))
    return np.moveaxis(v, -1, 0)


def make_in_maps(inputs, L, ncores):
    x = np.asarray(inputs["x"], np.float32)
    c = np.asarray(inputs["c"], np.float32)
    ctx = np.asarray(inputs["ctx"], np.float32)
    c_ctx = np.asarray(inputs["c_ctx"], np.float32)
    soff, NSM = small_layout(L)
    LLb = build_LL(np.asarray(inputs["rpb"], np.float32))
    w_ada = np.ascontiguousarray(inputs["w_ada"], dtype=np.float32)
    w_in = np.ascontiguousarray(inputs["w_in"], dtype=np.float32)
    w_out = np.ascontiguousarray(inputs["w_out"], dtype=np.float32)
    fg = np.ascontiguousarray(np.asarray(inputs["final_g"], np.float32).reshape(1, D))
    base = np.zeros((P, NSM), np.float32)

    def put(name, arr):
        arr = np.asarray(arr, np.float32).reshape(P, -1)
        base[:, soff[name]:soff[name] + arr.shape[1]] = arr
    put("normgT", fm(inputs["norm_g"], 8))
    put("badaT", fm(inputs["b_ada"], 24))
    put("convscT", np.transpose(fm(inputs["conv_sc"], 2), (0, 1, 3, 2)))
    put("convcfT", np.transpose(fm(inputs["conv_cf"], 2), (0, 1, 3, 2)))
    put("cfbT", fm(inputs["conv_cf_b"], 2))
    put("lngT", fm(inputs["ln_cf_g"], 2))
    put("lnbT", fm(inputs["ln_cf_b"], 2))
    put("ident", np.eye(P, dtype=np.float32))
    in_maps = []
    for i in range(ncores):
        sm = base.copy()
        cv = np.stack([c[2 * i], c[2 * i + 1], c_ctx], axis=0)
        cT = np.transpose(cv.reshape(3, 8, P), (2, 1, 0))
        sm[:, soff["cT"]:soff["cT"] + 24] = cT.reshape(P, 24)
        in_maps.append({
            "x": np.ascontiguousarray(x[2 * i:2 * i + 2]),
            "ctx": np.ascontiguousarray(ctx[2 * i:2 * i + 2]),
            "small": sm,
            "w_ada": w_ada, "w_in": w_in, "w_out": w_out, "LLb": LLb, "final_g": fg,
        })
    return in_maps


_CACHE = {}


def kernel(**inputs):
    L = int(np.asarray(inputs["w_in"]).shape[0])
    seq = int(np.asarray(inputs["x"]).shape[1])
    nb = int(np.asarray(inputs["x"]).shape[0])
    ncores = nb // 2
    key = (seq, L)
    if key not in _CACHE:
        _CACHE[key] = build_program(seq // GW, L)[0]
    nc = _CACHE[key]
    in_maps = make_in_maps(inputs, L, ncores)
    res = run_bass_kernel_spmd(nc, in_maps, core_ids=list(range(ncores)))
    return np.concatenate([np.asarray(r["out"]) for r in res.results], axis=0).astype(np.float32)
```
